# Optimizing a Trainium2 kernel written in Bass

```python
import math
import jax, jax.numpy as jnp
from jax import lax
import numpy as np

D_MODEL = 1024
BATCH = 8
SEQ = 2048
DEPTH = 4

N_META = 16
MIX_W = D_MODEL
ATT_W = MIX_W // 2
REC_W = MIX_W - ATT_W
ATT_HEAD_DIM = 64
N_ATT_HEADS = ATT_W // ATT_HEAD_DIM
N_REC_BLOCKS = 8
REC_BLOCK = REC_W // N_REC_BLOCKS
CONV_WIDTH = 4
RG_C = 8.0
D_FF = 4 * D_MODEL
Q_BLOCK = 128
NORM_EPS = 1e-6
D_IN = 3 * ATT_W + N_ATT_HEADS + 2 * REC_W

kernel_name = "hymba_fox_rglru_hybrid"


def rmsnorm(x, g):
    xf = x.astype(jnp.float32)
    y = xf * lax.rsqrt(jnp.mean(xf * xf, axis=-1, keepdims=True) + NORM_EPS)
    return (y * g.astype(jnp.float32)).astype(x.dtype)


def forgetting_attention(q, k, v, log_f):
    T = q.shape[1]
    scale = 1.0 / math.sqrt(q.shape[-1])
    c = jnp.cumsum(log_f, axis=1)
    c_bh = jnp.transpose(c, (0, 2, 1))
    starts = [0] + list(range(N_META, T, Q_BLOCK))
    ends = starts[1:] + [T]
    pos = jnp.arange(T)
    outs = []
    for qs, qe in zip(starts, ends):
        qb = q[:, qs:qe]
        kb = k[:, :qe]
        vb = v[:, :qe]
        s = jnp.einsum('bqhd,bkhd->bhqk', qb, kb).astype(jnp.float32) * scale
        bias = c_bh[:, :, qs:qe, None] - c_bh[:, :, None, :qe]
        mask = pos[qs:qe, None] >= pos[None, :qe]
        s = jnp.where(mask[None, None], s + bias, -jnp.inf)
        p = jax.nn.softmax(s, axis=-1).astype(vb.dtype)
        outs.append(jnp.einsum('bhqk,bkhd->bqhd', p, vb))
    return jnp.concatenate(outs, axis=1)


def causal_depthwise_conv(x, w, b):
    C = x.shape[-1]
    y = lax.conv_general_dilated(
        x, w[:, None, :].astype(x.dtype), window_strides=(1,),
        padding=[(CONV_WIDTH - 1, 0)], dimension_numbers=('NWC', 'WIO', 'NWC'),
        feature_group_count=C)
    return y + b.astype(x.dtype)


def block_diag_linear(x, w, b):
    B, T, C = x.shape
    xb = x.reshape(B, T, N_REC_BLOCKS, REC_BLOCK)
    y = jnp.einsum('btnd,nde->btne', xb, w.astype(x.dtype)).reshape(B, T, C)
    return y + b.astype(x.dtype)


def rg_lru(x, w_ga, b_ga, w_gx, b_gx, lru_L):
    r = jax.nn.sigmoid(block_diag_linear(x, w_ga, b_ga).astype(jnp.float32))
    i = jax.nn.sigmoid(block_diag_linear(x, w_gx, b_gx).astype(jnp.float32))
    log_a = RG_C * r * jax.nn.log_sigmoid(lru_L.astype(jnp.float32))
    a = jnp.exp(log_a)
    mult = jnp.sqrt(-jnp.expm1(2.0 * log_a))
    u = mult * i * x.astype(jnp.float32)

    def combine(e1, e2):
        a1, b1 = e1
        a2, b2 = e2
        return a1 * a2, a2 * b1 + b2

    _, h = lax.associative_scan(combine, (a, u), axis=1)
    return h.astype(x.dtype)


def setup_inputs(seed: int = 0) -> dict:
    key = jax.random.key(seed)
    ks = jax.random.split(key, 24)
    f32 = jnp.float32
    L = DEPTH
    nrm = lambda k, shape, s: (jax.random.normal(k, shape, f32) * s)
    x = jax.random.normal(ks[0], (BATCH, SEQ, D_MODEL), f32)
    meta = nrm(ks[1], (N_META, D_MODEL), 1.0)
    attn_norm_g = 1.0 + nrm(ks[2], (L, D_MODEL), 0.02)
    w_in = nrm(ks[3], (L, D_MODEL, D_IN), D_MODEL ** -0.5)
    b_f = jax.random.uniform(ks[4], (L, N_ATT_HEADS), f32, 1.0, 5.0)
    conv_w = nrm(ks[5], (L, CONV_WIDTH, REC_W), CONV_WIDTH ** -0.5)
    conv_b = nrm(ks[6], (L, REC_W), 0.02)
    w_gate_a = nrm(ks[7], (L, N_REC_BLOCKS, REC_BLOCK, REC_BLOCK), REC_BLOCK ** -0.5)
    b_gate_a = nrm(ks[8], (L, REC_W), 0.02)
    w_gate_x = nrm(ks[9], (L, N_REC_BLOCKS, REC_BLOCK, REC_BLOCK), REC_BLOCK ** -0.5)
    b_gate_x = nrm(ks[10], (L, REC_W), 0.02)
    u = jax.random.uniform(ks[11], (L, REC_W), f32, 0.9, 0.999)
    a_base = u ** (1.0 / RG_C)
    lru_L = jnp.log(a_base) - jnp.log1p(-a_base)
    attn_out_g = 1.0 + nrm(ks[12], (L, ATT_W), 0.02)
    rec_out_g = 1.0 + nrm(ks[13], (L, REC_W), 0.02)
    w_out = nrm(ks[14], (L, MIX_W, D_MODEL), (MIX_W * 2 * DEPTH) ** -0.5)
    mlp_norm_g = 1.0 + nrm(ks[15], (L, D_MODEL), 0.02)
    w_up = nrm(ks[16], (L, D_MODEL, D_FF), D_MODEL ** -0.5)
    w_down = nrm(ks[17], (L, D_FF, D_MODEL), (D_FF * 2 * DEPTH) ** -0.5)
    final_g = 1.0 + nrm(ks[18], (D_MODEL,), 0.02)
    return {"x": x, "meta": meta, "attn_norm_g": attn_norm_g, "w_in": w_in,
            "b_f": b_f, "conv_w": conv_w, "conv_b": conv_b,
            "w_gate_a": w_gate_a, "b_gate_a": b_gate_a,
            "w_gate_x": w_gate_x, "b_gate_x": b_gate_x, "lru_L": lru_L,
            "attn_out_g": attn_out_g, "rec_out_g": rec_out_g, "w_out": w_out,
            "mlp_norm_g": mlp_norm_g, "w_up": w_up, "w_down": w_down,
            "final_g": final_g}


def reference(x, meta, attn_norm_g, w_in, b_f, conv_w, conv_b, w_gate_a, b_gate_a,
              w_gate_x, b_gate_x, lru_L, attn_out_g, rec_out_g, w_out,
              mlp_norm_g, w_up, w_down, final_g):
    B, S, D = x.shape
    meta_b = jnp.broadcast_to(meta.astype(x.dtype)[None], (B, N_META, D))
    h = jnp.concatenate([meta_b, x], axis=1)
    T = h.shape[1]
    splits = np.cumsum([ATT_W, ATT_W, ATT_W, N_ATT_HEADS, REC_W]).tolist()
    for l in range(DEPTH):
        z = rmsnorm(h, attn_norm_g[l])
        proj = z @ w_in[l]
        q, k, v, f_logit, xr, yr = jnp.split(proj, splits, axis=-1)
        log_f = jax.nn.log_sigmoid(f_logit.astype(jnp.float32) + b_f[l].astype(jnp.float32))
        qh = q.reshape(B, T, N_ATT_HEADS, ATT_HEAD_DIM)
        kh = k.reshape(B, T, N_ATT_HEADS, ATT_HEAD_DIM)
        vh = v.reshape(B, T, N_ATT_HEADS, ATT_HEAD_DIM)
        attn = forgetting_attention(qh, kh, vh, log_f).reshape(B, T, ATT_W)
        xc = causal_depthwise_conv(xr, conv_w[l], conv_b[l])
        hr = rg_lru(xc, w_gate_a[l], b_gate_a[l], w_gate_x[l], b_gate_x[l], lru_L[l])
        rec = hr * jax.nn.gelu(yr)
        mix = jnp.concatenate([rmsnorm(attn, attn_out_g[l]), rmsnorm(rec, rec_out_g[l])], axis=-1)
        h = h + mix @ w_out[l]
        z = rmsnorm(h, mlp_norm_g[l])
        u = jax.nn.relu(z @ w_up[l])
        h = h + (u * u) @ w_down[l]
    h = rmsnorm(h, final_g)
    return h[:, N_META:]
```

```python
import numpy as np
import concourse.bass as bass
import concourse.mybir as mybir
from concourse.bass_utils import run_bass_kernel_spmd

F32 = mybir.dt.float32
BF16 = mybir.dt.bfloat16
AF = mybir.ActivationFunctionType
ALU = mybir.AluOpType

DEPTH = 4
NL_PER_LAUNCH = 4
T = 2064
S = 2048
D = 1024
NMETA = 16
DIN = 2568
DFF = 4096
EPS = 1e-6
NPP = 60
TT512 = [(0, 512), (512, 512), (1024, 512), (1536, 512), (2048, 16)]
TT256 = [(256 * j, 256) for j in range(8)] + [(2048, 16)]
KT = [(128 * i, 128) for i in range(16)] + [(2048, 16)]
NWB = 5
WCOLS = 256


class Tile:
    __slots__ = ("name", "last_w", "readers", "excl")

    def __init__(self, name, excl=False):
        self.name = name
        self.last_w = None
        self.readers = {}
        self.excl = excl


class Trk:
    def __init__(self, nc, ndma=16):
        self.nc = nc
        self.eng = {"pe": nc.tensor, "act": nc.scalar, "dve": nc.vector, "pool": nc.gpsimd, "sp": nc.sync}
        self.sem = {k: nc.alloc_semaphore("s_" + k) for k in self.eng}
        self.cnt = {k: 0 for k in self.eng}
        self.dsem = [nc.alloc_semaphore("d_%d" % i) for i in range(ndma)]
        self.dcnt = [0] * ndma
        self.dpool = {"pool": list(range(0, ndma // 2)), "sp": list(range(ndma // 2, ndma))}
        self.dnext = {"pool": 0, "sp": 0}
        self.waited = {}
        self.nwaits = 0
        self.pend = {k: False for k in self.eng}

    def _wait(self, e, ev):
        key = (e, ev[0])
        if self.waited.get(key, 0) >= ev[1]:
            return
        self.waited[key] = ev[1]
        s = self.sem[ev[0]] if isinstance(ev[0], str) else self.dsem[ev[0]]
        self.eng[e].wait_ge(s, ev[1])
        self.nwaits += 1

    def _deps(self, e, reads, writes):
        deps = {}

        def add(k, v):
            if deps.get(k, 0) < v:
                deps[k] = v

        for t in reads:
            if t.last_w is not None:
                add(*t.last_w)
            if t.excl:
                for k, v in t.readers.items():
                    if k != e:
                        add(k, v)
        for t in writes:
            if t.last_w is not None:
                add(*t.last_w)
            for k, v in t.readers.items():
                add(k, v)
        for k, v in deps.items():
            if k == "pe" and e == "pe":
                continue
            self._wait(e, (k, v))

    def _upd(self, ev, reads, writes):
        for t in reads:
            if t.readers.get(ev[0], 0) < ev[1]:
                t.readers[ev[0]] = ev[1]
        for t in writes:
            t.last_w = ev
            t.readers = {}

    def op(self, e, fn, reads=(), writes=(), mark=True):
        self._deps(e, reads, writes)
        inst = fn(self.eng[e])
        if mark:
            self.cnt[e] += 1
            inst.then_inc(self.sem[e], 1)
            ev = (e, self.cnt[e])
            self.pend[e] = False
        else:
            ev = (e, self.cnt[e] + 1)
            self.pend[e] = True
        self._upd(ev, reads, writes)
        return inst

    def dma(self, e, fn, reads=(), writes=()):
        self._deps(e, reads, writes)
        pl = self.dpool[e]
        k = pl[self.dnext[e] % len(pl)]
        self.dnext[e] += 1
        if self.dcnt[k] > 0:
            self._wait(e, (k, self.dcnt[k]))
        inst = fn(self.eng[e])
        self.dcnt[k] += 16
        inst.then_inc(self.dsem[k], 16)
        self._upd((k, self.dcnt[k]), reads, writes)
        return inst

    def snapshot(self):
        snap = {}
        for e in self.eng:
            v = self.cnt[e] + (1 if self.pend[e] else 0)
            if v > 0:
                snap[e] = v
        for k in range(len(self.dsem)):
            if self.dcnt[k] > 0:
                snap[k] = self.dcnt[k]
        return snap

    def barrier(self):
        for e in self.eng:
            for e2 in self.eng:
                if e2 != e and self.cnt[e2] > 0:
                    self._wait(e, (e2, self.cnt[e2]))
            for k in range(len(self.dsem)):
                if self.dcnt[k] > 0:
                    self._wait(e, (k, self.dcnt[k]))


class Arena:
    def __init__(self, ap_f32, nbytes):
        self.base = ap_f32
        self.nbytes = nbytes
        self.off = 0
        self.snap = {}

    def reset(self, tk=None):
        self.off = 0
        self.snap = tk.snapshot() if tk is not None else {}

    def take(self, shape, dtype, name):
        esz = 4 if dtype == F32 else 2
        n = 1
        for s in shape[1:]:
            n *= s
        nb = (n * esz + 31) // 32 * 32
        assert self.off + nb <= self.nbytes, (name, self.off, nb, self.nbytes)
        a = self.base[:, self.off // 4:(self.off + nb) // 4]
        self.off += nb
        if dtype != F32:
            a = a.bitcast(dtype)
        a = a[:, 0:n]
        if len(shape) == 3:
            a = a.rearrange("p (a b) -> p a b", b=shape[2])
        t = Tile(name)
        t.readers = dict(self.snap)
        return a, t


def build(nl, last, dbg=False, upto=99):
    nc = bass.Bass("TRN2", target_bir_lowering=False)
    hin = nc.dram_tensor("hin", [T, D], F32, kind="ExternalInput").ap()
    w_in = nc.dram_tensor("w_in", [nl, D, DIN], F32, kind="ExternalInput").ap()
    w_out = nc.dram_tensor("w_out", [nl, D, D], F32, kind="ExternalInput").ap()
    w_up = nc.dram_tensor("w_up", [nl, D, DFF], F32, kind="ExternalInput").ap()
    w_dn = nc.dram_tensor("w_dn", [nl, DFF, D], F32, kind="ExternalInput").ap()
    w_ga = nc.dram_tensor("w_ga", [nl, 8, 64, 64], F32, kind="ExternalInput").ap()
    w_gx = nc.dram_tensor("w_gx", [nl, 8, 64, 64], F32, kind="ExternalInput").ap()
    ppd = nc.dram_tensor("pp", [nl, 128, NPP], F32, kind="ExternalInput").ap()
    bfd = nc.dram_tensor("b_f", [nl, 8], F32, kind="ExternalInput").ap()
    gad = nc.dram_tensor("g_a", [nl, 512], F32, kind="ExternalInput").ap()
    nrows_out = S if last else T
    hout = nc.dram_tensor("hout", [nrows_out, D], F32, kind="ExternalOutput").ap()
    dbg_out = {}
    if dbg:
        dbg_out["d_zt"] = nc.dram_tensor("d_zt", [128, 8, T], BF16, kind="ExternalOutput").ap()
        dbg_out["d_mr"] = nc.dram_tensor("d_mr", [128, 4, T], BF16, kind="ExternalOutput").ap()
        dbg_out["d_qk"] = nc.dram_tensor("d_qk", [128, 8, T], BF16, kind="ExternalOutput").ap()
        dbg_out["d_ma"] = nc.dram_tensor("d_ma", [128, 8, T], BF16, kind="ExternalOutput").ap()
        dbg_out["d_h1"] = nc.dram_tensor("d_h1", [128, 8, T], F32, kind="ExternalOutput").ap()
        dbg_out["d_h2"] = nc.dram_tensor("d_h2", [128, 8, T], F32, kind="ExternalOutput").ap()

    tk = Trk(nc)
    op, dma = tk.op, tk.dma

    def sb(name, shape, dt):
        return nc.alloc_sbuf_tensor(name, shape, dt).ap()

    H = sb("H", [128, 8, T], F32)
    ZT = sb("ZT", [128, 8, T], BF16)
    QK = sb("QK", [128, 8, T], BF16)
    MR = sb("MR", [128, 4, T], BF16)
    tH = [[Tile("H%d_%d" % (c, j)) for j in range(9)] for c in range(8)]
    tZ = [[Tile("Z%d_%d" % (c, j)) for j in range(9)] for c in range(8)]
    tQ = [[Tile("Q%d_%d" % (c, j)) for j in range(9)] for c in range(8)]
    tM = [[Tile("M%d_%d" % (c, j)) for j in range(9)] for c in range(4)]

    def tl(tiles, c, t0, n):
        j0 = t0 // 256
        j1 = (t0 + n - 1) // 256
        return [tiles[c][j] for j in range(j0, j1 + 1)]

    def tls(tiles, cs, t0, n):
        r = []
        for c in cs:
            r += tl(tiles, c, t0, n)
        return r

    WB = [sb("WB%d" % i, [128, 8, WCOLS], BF16) for i in range(NWB)]
    tWB = [Tile("WB%d" % i) for i in range(NWB)]
    ones_bf = sb("ones_bf", [128, 128], BF16)
    ones_f = sb("ones_f", [128, 128], F32)
    tri_f = sb("tri_f", [128, 128], F32)
    tri_bf = sb("tri_bf", [128, 128], BF16)
    id_f = sb("id_f", [128, 128], F32)
    id_bf = sb("id_bf", [128, 128], BF16)
    EPSB = sb("EPSB", [128, 1], F32)
    tC = Tile("consts")
    PPt = [sb("PP%d" % i, [128, NPP], F32) for i in range(2)]
    BFt = [sb("BF%d" % i, [128, 8], F32) for i in range(2)]
    GAt1 = sb("GAg", [128, 512], F32)
    GAt = [GAt1, GAt1]
    tGAt = Tile("GAt")
    GA = [sb("GA%d" % i, [128, 4, 128], BF16) for i in range(2)]
    GX = [sb("GX%d" % i, [128, 4, 128], BF16) for i in range(2)]
    WF = [sb("WF%d" % i, [128, 8, 8], BF16) for i in range(2)]
    tPar = [Tile("par%d" % i) for i in range(2)]
    SM = sb("SM", [128, 64], F32)
    tSM = Tile("SM")
    CARRY = sb("CARRY", [128, 4, 3], F32)
    HL = sb("HL", [128, 4], F32)
    tCar = [Tile("car%d" % c) for c in range(4)]
    tHL = [Tile("hl%d" % c) for c in range(4)]
    NSQ = [(sb("nsq%d" % i, [128, 256], BF16), Tile("nsq%d" % i)) for i in range(2)]
    NRS = [(sb("nrs%d" % i, [128, 256], F32), Tile("nrs%d" % i)) for i in range(1)]
    nstate = {"k": 0, "r": 0}
    ARENA_BYTES = (nc.sbuf_bytes_remaining - 64) // 32 * 32
    print("ARENA_BYTES", ARENA_BYTES)
    AR = Arena(sb("ARENA", [128, ARENA_BYTES // 4], F32), ARENA_BYTES)

    PS = [nc.alloc_psum_tensor("ps%d" % i, [128, 512], F32).ap() for i in range(8)]
    tPS = [Tile("ps%d" % i, excl=True) for i in range(8)]

    class Rot:
        def __init__(self, idxs):
            self.idxs = idxs
            self.k = 0

        def next(self):
            i = self.idxs[self.k % len(self.idxs)]
            self.k += 1
            return PS[i], tPS[i]

    op("pool", lambda e: e.memset(ones_bf, 1.0), writes=[tC])
    op("pool", lambda e: e.memset(ones_f, 1.0), writes=[tC])
    op("pool", lambda e: e.memset(EPSB, EPS), writes=[tC])
    for a in (tri_f, tri_bf):
        op("pool", lambda e: e.memset(a, 1.0), writes=[tC])
        op("pool", lambda e: e.affine_select(out=a, in_=a, pattern=[[1, 128]], compare_op=ALU.is_ge, fill=0.0,
                                             base=0, channel_multiplier=-1), reads=[tC], writes=[tC])
    for a in (id_f, id_bf):
        op("pool", lambda e: e.memset(a, 1.0), writes=[tC])
        op("pool", lambda e: e.affine_select(out=a, in_=a, pattern=[[1, 128]], compare_op=ALU.is_equal, fill=0.0,
                                             base=0, channel_multiplier=-1), reads=[tC], writes=[tC])
    for i in range(2):
        op("pool", lambda e: e.memset(GA[i], 0.0), writes=[tPar[i]])
        op("pool", lambda e: e.memset(GX[i], 0.0), writes=[tPar[i]])

    units = []

    def wview(wt, l, r0, c0):
        return wt[l, r0:r0 + 1024, c0:c0 + WCOLS].rearrange("(kc p) c -> p kc c", p=128)

    U = {}
    for l in range(nl):
        for name, base in (("xr", 1544), ("yr", 2056), ("q", 0), ("v", 1024), ("k", 512)):
            for u in range(2):
                U[(l, name, u)] = len(units)
                units.append(wview(w_in, l, 0, base + u * WCOLS))
        for u in range(4):
            U[(l, "o", u)] = len(units)
            units.append(wview(w_out, l, 0, u * WCOLS))
        for g in range(4):
            for u in range(4):
                U[(l, "up", g, u)] = len(units)
                units.append(wview(w_up, l, 0, g * 1024 + u * WCOLS))
            for u in range(4):
                U[(l, "dn", g, u)] = len(units)
                units.append(wview(w_dn, l, g * 1024, u * WCOLS))
    wstate = {"issued": 0}

    def wadvance(oldest):
        while wstate["issued"] < min(len(units), oldest + NWB):
            j = wstate["issued"]
            b = j % NWB
            src = units[j]
            dma("pool", lambda e: e.dma_start(out=WB[b], in_=src), writes=[tWB[b]])
            wstate["issued"] += 1

    def wunit(j):
        assert j < wstate["issued"] and j >= wstate["issued"] - NWB, (j, wstate["issued"])
        return WB[j % NWB], tWB[j % NWB]

    def load_params(l):
        i = l % 2
        dma("sp", lambda e: e.dma_start(out=PPt[i], in_=ppd[l]), writes=[tPar[i]])
        dma("sp", lambda e: e.dma_start(out=BFt[i], in_=bfd[l:l + 1, :].partition_broadcast(128)), writes=[tPar[i]])
        dma("sp", lambda e: e.dma_start(out=GAt[i], in_=gad[l:l + 1, :].partition_broadcast(128)), writes=[tGAt])
        for q in range(2):
            for (dst, srcw) in ((GA[i], w_ga), (GX[i], w_gx)):
                for nn in range(4):
                    dma("pool", lambda e: e.dma_start(out=dst[64 * q:64 * q + 64, nn, 64 * q:64 * q + 64],
                                                      in_=srcw[l, 2 * nn + q]), writes=[tPar[i]])
        dma("pool", lambda e: e.dma_start(
            out=WF[i], in_=w_in[l, :, 1536:1544].rearrange("(kc p) c -> p kc c", p=128)), writes=[tPar[i]])

    import threading
    coop = {"cur": None}

    def cop(*a, **k):
        r = op(*a, **k)
        th = coop["cur"]
        if th is not None:
            coop["main"].set()
            th["evt"].wait()
            th["evt"].clear()
            coop["cur"] = th
        return r

    def cidle(n):
        th = coop["cur"]
        for _ in range(n):
            if th is not None:
                coop["main"].set()
                th["evt"].wait()
                th["evt"].clear()
                coop["cur"] = th

    def coop_run(fns, width, stagger=0, bg=None):
        main_evt = threading.Event()
        coop["main"] = main_evt
        pending = list(fns)
        slots = [None] * (width + (1 if bg is not None else 0))
        errs = []

        def start(fn, slot):
            th = {"evt": threading.Event(), "done": False}

            def body():
                th["evt"].wait()
                th["evt"].clear()
                coop["cur"] = th
                try:
                    fn(slot)
                except BaseException as ex:
                    errs.append(ex)
                finally:
                    th["done"] = True
                    coop["cur"] = None
                    main_evt.set()

            th["t"] = threading.Thread(target=body)
            th["t"].start()
            return th

        rnd = 0
        started = 0
        if bg is not None:
            slots[width] = start(bg, width)
        while pending or any(x is not None for x in slots):
            rnd += 1
            for si in range(len(slots)):
                if si < width and slots[si] is None and pending and (started >= width or rnd > started * stagger):
                    slots[si] = start(pending.pop(0), si)
                    started += 1
                th = slots[si]
                if th is None:
                    continue
                th["evt"].set()
                main_evt.wait()
                main_evt.clear()
                if errs:
                    raise errs[0]
                if th["done"]:
                    th["t"].join()
                    slots[si] = None
        coop["cur"] = None

    def phase_load(with_norm=False):
        rotn = Rot([4, 5])
        AR.reset(tk)
        st = [AR.take([128, 1024], F32, "st%d" % i) for i in range(2)]
        rot = Rot([0, 1, 2, 3])
        for i, (t0, n) in enumerate(KT):
            sa, stl = st[i % 2]
            dma("sp", lambda e: e.dma_start(out=sa[0:n, :], in_=hin[t0:t0 + n, :]), writes=[stl])
            for g in range(2):
                ps, tps = rot.next()
                ps3 = ps.rearrange("p (a b) -> p a b", b=128)
                for j in range(4):
                    kc = 4 * g + j
                    op("pe", lambda e: e.transpose(ps3[:, j, 0:n], sa[0:n, kc * 128:(kc + 1) * 128], id_f[0:n, 0:n]),
                       reads=[stl, tC], writes=[tps], mark=(j == 3))
                eng = "act" if g == 0 else "dve"
                if eng == "act":
                    op("act", lambda e: e.activation(out=H[:, 4 * g:4 * g + 4, t0:t0 + n], in_=ps3[:, :, 0:n], func=AF.Copy),
                       reads=[tps], writes=tls(tH, range(4 * g, 4 * g + 4), t0, n))
                else:
                    op("dve", lambda e: e.tensor_copy(out=H[:, 4 * g:4 * g + 4, t0:t0 + n], in_=ps3[:, :, 0:n]),
                       reads=[tps], writes=tls(tH, range(4 * g, 4 * g + 4), t0, n))
            if with_norm and (i % 2 == 1 or i == 16):
                norm_tile(i // 2, 0, 36, rotn)


    def norm_tile(j, par, gcol, rot):
        t0, n = TT256[j]
        ps, tps = rot.next()
        for kc in range(8):
            sqa, sqt = NSQ[nstate["k"] % 2]
            nstate["k"] += 1
            op("act", lambda e: e.activation(out=sqa[:, 0:n], in_=H[:, kc, t0:t0 + n], func=AF.Square),
               reads=tl(tH, kc, t0, n), writes=[sqt])
            op("pe", lambda e: e.matmul(ps[:, 0:n], lhsT=ones_bf, rhs=sqa[:, 0:n], start=(kc == 0), stop=(kc == 7)),
               reads=[sqt, tC], writes=[tps])
        ra, rt = NRS[0]
        nstate["r"] += 1
        op("act", lambda e: e.activation(out=ra[:, 0:n], in_=ps[:, 0:n], func=AF.Ln, bias=EPSB, scale=1.0 / D), reads=[tps, tC], writes=[rt])
        op("act", lambda e: e.activation(out=ra[:, 0:n], in_=ra[:, 0:n], func=AF.Exp, scale=-0.5), reads=[rt], writes=[rt])
        for kc in range(8):
            op("dve", lambda e: e.scalar_tensor_tensor(out=ZT[:, kc, t0:t0 + n], in0=H[:, kc, t0:t0 + n],
                                                       scalar=PPt[par][:, gcol + kc:gcol + kc + 1], in1=ra[:, 0:n],
                                                       op0=ALU.mult, op1=ALU.mult),
               reads=tl(tH, kc, t0, n) + [rt, tPar[par]], writes=tl(tZ, kc, t0, n))

    def norm_range(t0, n, par, gcol, rot, done):
        while done[0] < len(TT256) and TT256[done[0]][0] + TT256[done[0]][1] <= t0 + n:
            norm_tile(done[0], par, gcol, rot)
            done[0] += 1

    def phase_norm(par, gcol):
        rot = Rot([0, 1])
        for j in range(len(TT256)):
            norm_tile(j, par, gcol, rot)

    def phase_rec(l, par):
        AR.reset(tk)
        NS = 6
        sets = []
        for s in range(NS):
            d = {}
            d["X"] = AR.take([128, 259], F32, "X%d" % s)
            d["C"] = AR.take([128, 256], F32, "C%d" % s)
            d["R"] = AR.take([128, 256], F32, "R%d" % s)
            d["I"] = AR.take([128, 256], F32, "I%d" % s)
            sets.append(d)
        REC = [AR.take([128, 4, 256], BF16, "REC%d" % i) for i in range(2)]
        RS = [AR.take([128, 256], F32, "RS0")] * 2
        PP = PPt[par]
        tP = tPar[par]
        op("act", lambda e: e.activation(out=SM[:, 16:20], in_=PP[:, 28:32], func=AF.Exp, scale=-1.0), reads=[tP], writes=[tSM])
        op("act", lambda e: e.activation(out=SM[:, 16:20], in_=SM[:, 16:20], func=AF.Ln, bias=1.0), reads=[tSM], writes=[tSM])
        op("dve", lambda e: e.tensor_scalar(out=SM[:, 0:4], in0=SM[:, 16:20], scalar1=-8.0, scalar2=None, op0=ALU.mult), reads=[tSM], writes=[tSM])
        op("dve", lambda e: e.tensor_scalar(out=SM[:, 4:8], in0=SM[:, 16:20], scalar1=-16.0, scalar2=None, op0=ALU.mult), reads=[tSM], writes=[tSM])
        op("dve", lambda e: e.tensor_scalar(out=SM[:, 8:16], in0=PP[:, 20:28], scalar1=-1.0, scalar2=None, op0=ALU.mult), reads=[tP, tSM], writes=[tSM])
        op("dve", lambda e: e.memset(CARRY, 0.0), writes=tCar)
        u0 = U[(l, "xr", 0)]
        wadvance(u0)
        Wxr = [wunit(u0), wunit(u0 + 1)]
        Wyr = [wunit(u0 + 2), wunit(u0 + 3)]
        rotP = Rot([0, 1, 2])
        rotG = Rot([3, 4, 5])
        psSs = [(PS[6], tPS[6]), (PS[7], tPS[7])]
        RECt = [[Tile("rec%d_%d" % (i, c)) for c in range(4)] for i in range(2)]
        for i_ in range(2):
            for t_ in RECt[i_]:
                t_.readers = dict(AR.snap)

        def chain(n, c0, slot):
            op = cop
            c = c0
            t0, N = TT256[n]
            reca = REC[n % 2][0]
            rect = RECt[n % 2][c]
            rsa, rst = RS[n % 2]
            psS, tpsS = psSs[n % 2]
            d = sets[slot]
            (X, tX), (C, tCc), (R, tR), (I, tI) = (d[k] for k in ("X", "C", "R", "I"))
            A, tA = R, tR
            CB, tCB = X[:, 0:128].bitcast(BF16), tX
            A2, tA2 = X, tX
            wa, wt = Wxr[c // 2]
            ps, tps = PS[slot], tPS[slot]
            for kc in range(8):
                op("pe", lambda e: e.matmul(ps[:, 0:N], lhsT=wa[:, kc, (c % 2) * 128:(c % 2) * 128 + 128],
                                            rhs=ZT[:, kc, t0:t0 + N], start=(kc == 0), stop=(kc == 7)),
                   reads=[wt] + tl(tZ, kc, t0, N), writes=[tps], mark=(kc == 7))
            op("act", lambda e: e.activation(out=X[:, 3:3 + N], in_=ps[:, 0:N], func=AF.Copy), reads=[tps], writes=[tX])
            op("dve", lambda e: e.tensor_copy(out=X[:, 0:3], in_=CARRY[:, c, :]), reads=[tCar[c]], writes=[tX])
            op("pool", lambda e: e.tensor_copy(out=CARRY[:, c, :], in_=X[:, N:N + 3]), reads=[tX], writes=[tCar[c]])
            op("dve", lambda e: e.tensor_scalar(out=C[:, 0:N], in0=X[:, 3:3 + N], scalar1=PP[:, c * 4 + 3:c * 4 + 4],
                                                scalar2=PP[:, 16 + c:17 + c], op0=ALU.mult, op1=ALU.add),
               reads=[tX, tP], writes=[tCc])
            for kk in (2, 1, 0):
                op("dve", lambda e: e.scalar_tensor_tensor(out=C[:, 0:N], in0=X[:, kk:kk + N],
                                                           scalar=PP[:, c * 4 + kk:c * 4 + kk + 1], in1=C[:, 0:N],
                                                           op0=ALU.mult, op1=ALU.add), reads=[tX, tP, tCc], writes=[tCc])
            op("pool", lambda e: e.tensor_copy(out=CB[:, 0:N], in_=C[:, 0:N]), reads=[tCc], writes=[tCB])
            psg, tpsa = PS[slot], tPS[slot]
            tpsi = tpsa
            psa = psg[:, 0:256]
            psi = psg[:, 256:512]
            op("pe", lambda e: e.matmul(psa[:, 0:N], lhsT=GA[par][:, c, :], rhs=CB[:, 0:N], start=True, stop=True),
               reads=[tCB, tP], writes=[tpsa])
            op("pe", lambda e: e.matmul(psi[:, 0:N], lhsT=GX[par][:, c, :], rhs=CB[:, 0:N], start=True, stop=True),
               reads=[tCB, tP], writes=[tpsi])
            op("act", lambda e: e.activation(out=R[:, 0:N], in_=psa[:, 0:N], func=AF.Exp, bias=SM[:, 8 + c:9 + c], scale=-1.0),
               reads=[tpsa, tSM], writes=[tR])
            op("act", lambda e: e.activation(out=I[:, 0:N], in_=psi[:, 0:N], func=AF.Exp, bias=SM[:, 12 + c:13 + c], scale=-1.0),
               reads=[tpsi, tSM], writes=[tI])
            op("act", lambda e: e.activation(out=R[:, 0:N], in_=R[:, 0:N], func=AF.Ln, bias=1.0), reads=[tR], writes=[tR])
            op("act", lambda e: e.activation(out=R[:, 0:N], in_=R[:, 0:N], func=AF.Exp, scale=-1.0), reads=[tR], writes=[tR])
            op("act", lambda e: e.activation(out=I[:, 0:N], in_=I[:, 0:N], func=AF.Ln, bias=1.0), reads=[tI], writes=[tI])
            op("act", lambda e: e.activation(out=A2[:, 0:N], in_=R[:, 0:N], func=AF.Exp, scale=SM[:, 4 + c:5 + c]), reads=[tR, tSM], writes=[tA2])
            op("act", lambda e: e.activation(out=A[:, 0:N], in_=R[:, 0:N], func=AF.Exp, scale=SM[:, c:c + 1]), reads=[tR, tSM], writes=[tA])
            op("act", lambda e: e.activation(out=A2[:, 0:N], in_=A2[:, 0:N], func=AF.Ln, bias=1.0, scale=-1.0), reads=[tA2], writes=[tA2])
            op("dve", lambda e: e.scalar_tensor_tensor(out=I[:, 0:N], in0=A2[:, 0:N], scalar=0.5, in1=I[:, 0:N], op0=ALU.mult, op1=ALU.subtract),
               reads=[tI, tA2], writes=[tI])
            op("act", lambda e: e.activation(out=I[:, 0:N], in_=I[:, 0:N], func=AF.Exp), reads=[tI], writes=[tI])
            op("dve", lambda e: e.tensor_tensor(out=I[:, 0:N], in0=I[:, 0:N], in1=C[:, 0:N], op=ALU.mult), reads=[tI, tCc], writes=[tI])
            init = 0.0 if n == 0 else HL[:, c:c + 1]
            op("dve", lambda e: e.tensor_tensor_scan(out=C[:, 0:N], data0=A[:, 0:N], data1=I[:, 0:N], initial=init, op0=ALU.mult, op1=ALU.add),
               reads=[tA, tI, tHL[c]], writes=[tCc])
            op("pool", lambda e: e.tensor_copy(out=HL[:, c:c + 1], in_=C[:, N - 1:N]), reads=[tCc], writes=[tHL[c]])
            wa, wt = Wyr[c // 2]
            ps, tps = PS[slot], tPS[slot]
            for kc in range(8):
                op("pe", lambda e: e.matmul(ps[:, 0:N], lhsT=wa[:, kc, (c % 2) * 128:(c % 2) * 128 + 128],
                                            rhs=ZT[:, kc, t0:t0 + N], start=(kc == 0), stop=(kc == 7)),
                   reads=[wt] + tl(tZ, kc, t0, N), writes=[tps], mark=(kc == 7))
            op("act", lambda e: e.activation(out=I[:, 0:N], in_=ps[:, 0:N], func=AF.Copy), reads=[tps], writes=[tI])
            op("dve", lambda e: e.tensor_tensor(out=X[:, 0:N], in0=I[:, 0:N], in1=I[:, 0:N], op=ALU.mult), reads=[tI], writes=[tX])
            op("dve", lambda e: e.tensor_scalar(out=X[:, 0:N], in0=X[:, 0:N], scalar1=0.044715, scalar2=1.0, op0=ALU.mult, op1=ALU.add),
               reads=[tX], writes=[tX])
            op("dve", lambda e: e.tensor_tensor(out=X[:, 0:N], in0=X[:, 0:N], in1=I[:, 0:N], op=ALU.mult), reads=[tX, tI], writes=[tX])
            op("act", lambda e: e.activation(out=X[:, 0:N], in_=X[:, 0:N], func=AF.Exp, scale=-2.0 * 0.7978845608028654), reads=[tX], writes=[tX])
            op("act", lambda e: e.activation(out=X[:, 0:N], in_=X[:, 0:N], func=AF.Ln, bias=1.0), reads=[tX], writes=[tX])
            op("act", lambda e: e.activation(out=X[:, 0:N], in_=X[:, 0:N], func=AF.Exp, scale=-1.0), reads=[tX], writes=[tX])
            op("dve", lambda e: e.tensor_tensor(out=I[:, 0:N], in0=I[:, 0:N], in1=C[:, 0:N], op=ALU.mult), reads=[tI, tCc], writes=[tI])
            op("dve", lambda e: e.tensor_tensor(out=reca[:, c, 0:N], in0=I[:, 0:N], in1=X[:, 0:N], op=ALU.mult), reads=[tI, tX], writes=[rect])
            op("act", lambda e: e.activation(out=CB[:, 0:N], in_=reca[:, c, 0:N], func=AF.Square), reads=[rect], writes=[tCB])
            op("pe", lambda e: e.matmul(psS[:, 0:N], lhsT=ones_bf, rhs=CB[:, 0:N], start=(c == 0), stop=(c == 3)),
               reads=[tCB, tC], writes=[tpsS])
            if c0 == 3:
                op("act", lambda e: e.activation(out=rsa[:, 0:N], in_=psS[:, 0:N], func=AF.Ln, bias=EPSB, scale=1.0 / 512), reads=[tpsS, tC], writes=[rst])
                op("act", lambda e: e.activation(out=rsa[:, 0:N], in_=rsa[:, 0:N], func=AF.Exp, scale=-0.5), reads=[rst], writes=[rst])
                for c in range(4):
                    op("dve", lambda e: e.scalar_tensor_tensor(out=MR[:, c, t0:t0 + N], in0=reca[:, c, 0:N], scalar=PP[:, 32 + c:33 + c],
                                                               in1=rsa[:, 0:N], op0=ALU.mult, op1=ALU.mult),
                       reads=[RECt[n % 2][c], rst, tP], writes=tl(tM, c, t0, N))

        def bg_q0(slot):
            wa, wt = wunit(U[(l, "q", 0)])
            rotb = Rot([5])
            for mm in range(2):
                for (t0, n) in TT512:
                    ps, tps = rotb.next()
                    for kc in range(8):
                        cop("pe", lambda e: e.matmul(ps[:, 0:n], lhsT=wa[:, kc, mm * 128:mm * 128 + 128], rhs=ZT[:, kc, t0:t0 + n],
                                                     start=(kc == 0), stop=(kc == 7)),
                            reads=[wt] + tl(tZ, kc, t0, n), writes=[tps], mark=(kc == 7))
                        cidle(3)
                    if mm % 2 == 0:
                        cop("act", lambda e: e.activation(out=QK[:, mm, t0:t0 + n], in_=ps[:, 0:n], func=AF.Copy),
                            reads=[tps], writes=tl(tQ, mm, t0, n))
                    else:
                        cop("dve", lambda e: e.tensor_copy(out=QK[:, mm, t0:t0 + n], in_=ps[:, 0:n]),
                            reads=[tps], writes=tl(tQ, mm, t0, n))
                    cidle(3)

        coop_run([(lambda slot, n=n, c=c: chain(n, c, slot)) for n in range(len(TT256)) for c in range(4)], NS, stagger=8)

    def phase_attn(l, par):
        AR.reset(tk)
        VP, _ = AR.take([128, 17, 520], BF16, "VP")
        tV = [Tile("V%d" % i) for i in range(17)]
        for t_ in tV:
            t_.readers = dict(AR.snap)
        Lt, tL = AR.take([128, 17, 8], F32, "L")
        Lp, tLp = AR.take([128, 18, 8], F32, "Lp")
        CT, tCT = AR.take([128, 35, 8], F32, "CT")
        boff = [sum((2 * jj + 2) for jj in range(J_)) for J_ in range(9)]
        BT, tBT = AR.take([128, 89, 8], F32, "BT")
        PT = [AR.take([128, 256], BF16, "PT%d" % i) for i in range(3)]
        AT = [AR.take([128, 512], F32, "AT%d" % i) for i in range(2)]
        AN = [AR.take([128, 512], BF16, "AN%d" % i) for i in range(2)]
        RR, tRR = AR.take([128, 8], F32, "RR")
        SSq, tSS = AR.take([128, 4], F32, "SSq")
        PP = PPt[par]
        tP = tPar[par]
        rot = Rot([0, 1, 2, 3])
        evs = {"ev": 0}

        def proj_qk(which, cbase):
            ev = evs["ev"]
            for u in range(2):
                j = U[(l, which, u)]
                wadvance(j)
                if which == "q" and u == 0 and qdone.get(l):
                    continue
                wa, wt = wunit(j)
                for mm in range(2):
                    m = 2 * u + mm
                    for (t0, n) in TT512:
                        ps, tps = rot.next()
                        for kc in range(8):
                            op("pe", lambda e: e.matmul(ps[:, 0:n], lhsT=wa[:, kc, mm * 128:mm * 128 + 128], rhs=ZT[:, kc, t0:t0 + n],
                                                        start=(kc == 0), stop=(kc == 7)),
                               reads=[wt] + tl(tZ, kc, t0, n), writes=[tps], mark=(kc == 7))
                        if m % 2 == 0:
                            op("act", lambda e: e.activation(out=QK[:, cbase + m, t0:t0 + n], in_=ps[:, 0:n], func=AF.Copy),
                               reads=[tps], writes=tl(tQ, cbase + m, t0, n))
                        else:
                            op("dve", lambda e: e.tensor_copy(out=QK[:, cbase + m, t0:t0 + n], in_=ps[:, 0:n]),
                               reads=[tps], writes=tl(tQ, cbase + m, t0, n))
                        ev += 1
            evs["ev"] = ev

        proj_qk("q", 0)
        jv = U[(l, "v", 0)]
        wadvance(jv)
        Wv = [wunit(jv), wunit(jv + 1)]
        VPo = VP.rearrange("p i (h e) -> p i h e", e=65)
        op("pool", lambda e: e.memset(VPo[:, :, :, 64:65], 1.0), writes=tV)
        op("pool", lambda e: e.memset(Lt, 0.0), writes=[tL])
        op("pool", lambda e: e.memset(Lp[:, 0, :], 0.0), writes=[tLp])
        psF, tpsF = PS[7], tPS[7]
        rotv = Rot([4, 5, 6])
        for i, (t0, nk) in enumerate(KT):
            for u in range(2):
                wa, wt = Wv[u]
                ps, tps = rotv.next()
                for kc in range(8):
                    op("pe", lambda e: e.matmul(ps[0:nk, 0:256], lhsT=ZT[:, kc, t0:t0 + nk], rhs=wa[:, kc, :], start=(kc == 0), stop=(kc == 7)),
                       reads=[wt] + tl(tZ, kc, t0, nk), writes=[tps], mark=(kc == 7))
                dst = VP[0:nk, i, u * 260:(u + 1) * 260].rearrange("p (h e) -> p h e", e=65)[:, :, 0:64]
                src = ps[0:nk, 0:256].rearrange("p (h d) -> p h d", d=64)
                if i % 2 == 0:
                    op("act", lambda e: e.activation(out=dst, in_=src, func=AF.Copy), reads=[tps], writes=[tV[i]])
                else:
                    op("dve", lambda e: e.tensor_copy(out=dst, in_=src), reads=[tps], writes=[tV[i]])
            for kc in range(8):
                op("pe", lambda e: e.matmul(psF[0:nk, i * 8:i * 8 + 8], lhsT=ZT[:, kc, t0:t0 + nk], rhs=WF[par][:, kc, :], start=(kc == 0), stop=(kc == 7)),
                   reads=[tP] + tl(tZ, kc, t0, nk), writes=[tpsF], mark=(kc == 7))
            op("dve", lambda e: e.tensor_tensor(out=Lt[0:nk, i, :], in0=psF[0:nk, i * 8:i * 8 + 8], in1=BFt[par][0:nk, :], op=ALU.add),
               reads=[tpsF, tP], writes=[tL])
        for (p1, i0, i1) in ((128, 0, 16), (16, 16, 17)):
            op("act", lambda e: e.activation(out=Lt[0:p1, i0:i1, :], in_=Lt[0:p1, i0:i1, :], func=AF.Exp, scale=-1.0), reads=[tL], writes=[tL])
            op("act", lambda e: e.activation(out=Lt[0:p1, i0:i1, :], in_=Lt[0:p1, i0:i1, :], func=AF.Ln, bias=1.0), reads=[tL], writes=[tL])
        for i in range(17):
            op("dve", lambda e: e.tensor_tensor(out=Lp[:, i + 1, :], in0=Lp[:, i, :], in1=Lt[:, i, :], op=ALU.add), reads=[tL, tLp], writes=[tLp])
        proj_qk("k", 4)
        psC, tpsC = PS[6], tPS[6]
        Ltf = Lt.rearrange("p i h -> p (i h)")
        Lpf = Lp.rearrange("p i h -> p (i h)")
        op("pe", lambda e: e.matmul(psC[:, 0:136], lhsT=tri_f, rhs=Ltf, start=True, stop=False), reads=[tL, tC], writes=[tpsC], mark=False)
        op("pe", lambda e: e.matmul(psC[:, 0:136], lhsT=ones_f, rhs=Lpf[:, 0:136], start=False, stop=True), reads=[tLp, tC], writes=[tpsC], mark=False)
        op("pe", lambda e: e.matmul(psC[:, 136:280], lhsT=ones_f, rhs=Lpf[:, 0:144], start=True, stop=True), reads=[tLp, tC], writes=[tpsC])
        CTf = CT.rearrange("p i h -> p (i h)")
        op("dve", lambda e: e.tensor_copy(out=CTf, in_=psC[:, 0:280]), reads=[tpsC], writes=[tCT])
        for J in range(9):
            ni = 2 * J + 2 if J < 8 else 17
            rJ = 17 + (2 * J + 1 if J < 8 else 17)
            for h in range(8):
                op("dve", lambda e: e.tensor_scalar(out=BT[:, boff[J]:boff[J] + ni, h], in0=CT[:, 0:ni, h], scalar1=CT[:, rJ, h:h + 1],
                                                    scalar2=None, op0=ALU.subtract), reads=[tCT], writes=[tBT])
        rotS = Rot([0, 1, 2])
        psT, tpsT = PS[7], tPS[7]
        psT_bf = psT.bitcast(BF16).rearrange("p (c t) -> p c t", t=128)
        osets = [[(PS[3], tPS[3]), (PS[4], tPS[4])], [(PS[5], tPS[5]), (PS[6], tPS[6])]]
        items = []
        hcount = 0
        for J in range(9):
            imax = 2 * J + 1 if J < 8 else 16
            for h in range(8):
                for i in range(imax + 1):
                    items.append((J, h, i, imax, hcount))
                hcount += 1
        LA = 2
        state = {}
        deferred = []

        def emit_S(t):
            J, h, i, imax, hc = items[t]
            hp, po = h // 2, 64 * (h % 2)
            q0 = 256 * J
            k0, nk = KT[i]
            if J < 8:
                blks = [0, 1] if i < imax else [1]
                nqb = 128
            else:
                blks = [0]
                nqb = 16
            qs = q0 + 128 * blks[0]
            nq = nqb * len(blks)
            psS, tpsS = rotS.next()
            op("pe", lambda e: e.matmul(psS[0:nk, 0:nq], lhsT=QK[po:po + 64, 4 + hp, k0:k0 + nk], rhs=QK[po:po + 64, hp, qs:qs + nq],
                                        start=True, stop=True),
               reads=tl(tQ, 4 + hp, k0, nk) + tl(tQ, hp, qs, nq), writes=[tpsS])
            state[t] = (psS, tpsS, blks, nqb, nq, nk)

        def epilogue_pe(J, b, nqb, tq0):
            ana, ant = AN[b]
            for c in range(4):
                op("pe", lambda e: e.transpose(psT_bf[:, c, 0:nqb], ana[0:nqb, c * 128:(c + 1) * 128], id_bf[0:nqb, 0:nqb]),
                   reads=[ant, tC], writes=[tpsT], mark=(c == 3))
            op("dve", lambda e: e.tensor_copy(out=ZT[:, 0:4, tq0:tq0 + nqb], in_=psT_bf[:, 0:4, 0:nqb]),
               reads=[tpsT], writes=tls(tZ, range(4), tq0, nqb))

        def emit_rest(t):
            J, h, i, imax, hc = items[t]
            psS, tpsS, blks, nqb, nq, nk = state.pop(t)
            oset = osets[hc % 2]
            pta, ptt = PT[t % 3]
            op("act", lambda e: e.activation(out=pta[0:nk, 0:nq], in_=psS[0:nk, 0:nq], func=AF.Exp,
                                             bias=BT[0:nk, boff[J] + i, h:h + 1], scale=0.125),
               reads=[tpsS, tBT], writes=[ptt])
            diag = (J < 8 and i >= 2 * J) or (J == 8 and i == 16)
            if diag:
                op("dve", lambda e: e.tensor_tensor(out=pta[0:nk, 0:nqb], in0=pta[0:nk, 0:nqb], in1=tri_bf[0:nk, 0:nqb], op=ALU.mult),
                   reads=[ptt, tC], writes=[ptt])
            for bi, b in enumerate(blks):
                po_, tpo = oset[b]
                last_i = (2 * J + b) if J < 8 else 16
                op("pe", lambda e: e.matmul(po_[0:nqb, 0:65], lhsT=pta[0:nk, bi * nqb:(bi + 1) * nqb],
                                            rhs=VP[0:nk, i, h * 65:(h + 1) * 65], start=(i == 0), stop=(i == last_i)),
                   reads=[ptt, tV[i]], writes=[tpo], mark=(i == last_i))
            if i == imax:
                nblk = 2 if J < 8 else 1
                for b in range(nblk):
                    po_, tpo = oset[b]
                    op("dve", lambda e: e.reciprocal(out=RR[0:nqb, h:h + 1], in_=po_[0:nqb, 64:65]), reads=[tpo], writes=[tRR])
                    op("dve", lambda e: e.tensor_scalar(out=AT[b][0][0:nqb, h * 64:(h + 1) * 64], in0=po_[0:nqb, 0:64], scalar1=RR[0:nqb, h:h + 1],
                                                        scalar2=None, op0=ALU.mult), reads=[tpo, tRR], writes=[AT[b][1]])
                if h == 7:
                    for b in range(nblk):
                        tq0 = 256 * J + 128 * b
                        ata_, att = AT[b]
                        ana, ant = AN[b]
                        op("act", lambda e: e.activation(out=ana[0:nqb, :], in_=ata_[0:nqb, :], func=AF.Square, accum_out=SSq[0:nqb, b:b + 1]),
                           reads=[att], writes=[ant, tSS])
                        op("act", lambda e: e.activation(out=SSq[0:nqb, b:b + 1], in_=SSq[0:nqb, b:b + 1], func=AF.Ln, bias=EPSB[0:nqb, :], scale=1.0 / 512),
                           reads=[tSS, tC], writes=[tSS])
                        op("act", lambda e: e.activation(out=SSq[0:nqb, b:b + 1], in_=SSq[0:nqb, b:b + 1], func=AF.Exp, scale=-0.5), reads=[tSS], writes=[tSS])
                        op("dve", lambda e: e.scalar_tensor_tensor(out=ana[0:nqb, :], in0=ata_[0:nqb, :], scalar=SSq[0:nqb, b:b + 1], in1=GAt[par][0:nqb, :],
                                                                   op0=ALU.mult, op1=ALU.mult), reads=[att, tSS, tGAt], writes=[ant])
                        deferred.append([4 + b, (J, b, nqb, tq0)])

        def tick():
            for d in deferred:
                d[0] -= 1
            while deferred and deferred[0][0] <= 0:
                _, a = deferred.pop(0)
                epilogue_pe(*a)

        n_items = len(items)
        for t in range(n_items + LA):
            if t < n_items:
                emit_S(t)
            if t - LA >= 0:
                emit_rest(t - LA)
            tick()
        while deferred:
            _, a = deferred.pop(0)
            epilogue_pe(*a)

    def phase_out(l, par, with_norm=True):
        j0 = U[(l, "o", 0)]
        wadvance(j0)
        Wo = [wunit(j0 + u) for u in range(4)]
        rot = Rot([0, 1, 2, 3])
        rotn = Rot([4, 5])
        def main(ti):
            t0, n = TT512[ti]
            for m in range(8):
                wa, wt = Wo[m // 2]
                mm = m % 2
                ps, tps = rot.next()
                for kc in range(8):
                    if kc < 4:
                        rhs, rt = ZT[:, kc, t0:t0 + n], tl(tZ, kc, t0, n)
                    else:
                        rhs, rt = MR[:, kc - 4, t0:t0 + n], tl(tM, kc - 4, t0, n)
                    op("pe", lambda e: e.matmul(ps[:, 0:n], lhsT=wa[:, kc, mm * 128:mm * 128 + 128], rhs=rhs, start=(kc == 0), stop=(kc == 7)),
                       reads=[wt] + rt, writes=[tps], mark=(kc == 7))
                op("dve", lambda e: e.tensor_tensor(out=H[:, m, t0:t0 + n], in0=ps[:, 0:n], in1=H[:, m, t0:t0 + n], op=ALU.add),
                   reads=[tps] + tl(tH, m, t0, n), writes=tl(tH, m, t0, n))

        ndone = [0]

        def stages(ti):
            t0, n = TT512[ti]
            return [lambda: norm_range(t0, n, par, 44, rotn, ndone)] if with_norm else []

        run_pipelined(len(TT512), main, stages)

    def phase_mlp(l, after_tile=None, setup=None):
        AR.reset(tk)
        RL = [AR.take([128, 512], F32, "RL%d" % i) for i in range(4)]
        if setup is not None:
            setup()
        rot = Rot([0, 1, 2, 3])
        rot2 = Rot([4, 5, 6, 7])
        k = 0
        for g in range(4):
            for u in range(4):
                j = U[(l, "up", g, u)]
                wadvance(j)
                wa, wt = wunit(j)
                for mm in range(2):
                    m = 2 * u + mm
                    for (t0, n) in TT512:
                        ps, tps = rot.next()
                        for kc in range(8):
                            op("pe", lambda e: e.matmul(ps[:, 0:n], lhsT=wa[:, kc, mm * 128:mm * 128 + 128], rhs=ZT[:, kc, t0:t0 + n],
                                                        start=(kc == 0), stop=(kc == 7)),
                               reads=[wt] + tl(tZ, kc, t0, n), writes=[tps], mark=(kc == 7))
                        ra, rt = RL[k % 4]
                        k += 1
                        op("act", lambda e: e.activation(out=ra[:, 0:n], in_=ps[:, 0:n], func=AF.Relu), reads=[tps], writes=[rt])
                        op("dve", lambda e: e.tensor_tensor(out=QK[:, m, t0:t0 + n], in0=ra[:, 0:n], in1=ra[:, 0:n], op=ALU.mult),
                           reads=[rt], writes=tl(tQ, m, t0, n))

            def down(m, t0, n, wa, wt, mm):
                ps, tps = rot2.next()
                for kc in range(8):
                    op("pe", lambda e: e.matmul(ps[:, 0:n], lhsT=wa[:, kc, mm * 128:mm * 128 + 128], rhs=QK[:, kc, t0:t0 + n],
                                                start=(kc == 0), stop=(kc == 7)),
                       reads=[wt] + tl(tQ, kc, t0, n), writes=[tps], mark=(kc == 7))
                op("dve", lambda e: e.tensor_tensor(out=H[:, m, t0:t0 + n], in0=ps[:, 0:n], in1=H[:, m, t0:t0 + n], op=ALU.add),
                   reads=[tps] + tl(tH, m, t0, n), writes=tl(tH, m, t0, n))

            if g < 3 or after_tile is None:
                for u in range(4):
                    j = U[(l, "dn", g, u)]
                    wadvance(j)
                    wa, wt = wunit(j)
                    for mm in range(2):
                        for (t0, n) in TT512:
                            down(2 * u + mm, t0, n, wa, wt, mm)
            else:
                jd = U[(l, "dn", g, 0)]
                wadvance(jd)
                Wd = [wunit(jd + u) for u in range(4)]
                def main(ti):
                    t0, n = TT512[ti]
                    for m in range(8):
                        wa, wt = Wd[m // 2]
                        down(m, t0, n, wa, wt, m % 2)

                run_pipelined(len(TT512), main, after_tile)

    sstate = {}
    qdone = {}

    def run_pipelined(ntiles, emit_main, stages_for):
        q = []

        def age():
            for ent in q:
                k = ent[0]
                if k < len(ent[1]):
                    ent[1][k]()
                ent[0] += 1

        for ti in range(ntiles):
            emit_main(ti)
            age()
            q.append([0, stages_for(ti)])
        while any(ent[0] < len(ent[1]) for ent in q):
            age()

    def store_setup(par, reset):
        if reset:
            AR.reset(tk)
        sstate["st"] = [AR.take([128, 1024], F32, "st%d" % i) for i in range(2)]
        sstate["rot"] = Rot([0, 1, 2])
        sstate["par"] = par
        sstate["k"] = 0
        sstate["bi"] = 0
        if last:
            sstate["sq"] = [AR.take([128, 512], BF16, "sq%d" % i) for i in range(3)]
            sstate["rs"] = [AR.take([128, 512], F32, "rs%d" % i) for i in range(2)]
            sstate["Y"] = [AR.take([128, 8, 128], F32, "Y%d" % i) for i in range(2)]
            sstate["rotn"] = Rot([3])

    def store_tile_a(ti):
        t0, n = TT512[ti]
        if last:
            ps, tps = sstate["rotn"].next()
            for kc in range(8):
                sqa, sqt = sstate["sq"][sstate["k"] % 3]
                sstate["k"] += 1
                op("act", lambda e: e.activation(out=sqa[:, 0:n], in_=H[:, kc, t0:t0 + n], func=AF.Square), reads=tl(tH, kc, t0, n), writes=[sqt])
                op("pe", lambda e: e.matmul(ps[:, 0:n], lhsT=ones_bf, rhs=sqa[:, 0:n], start=(kc == 0), stop=(kc == 7)),
                   reads=[sqt, tC], writes=[tps])
            ra, rt = sstate["rs"][ti % 2]
            op("act", lambda e: e.activation(out=ra[:, 0:n], in_=ps[:, 0:n], func=AF.Ln, bias=EPSB, scale=1.0 / D), reads=[tps, tC], writes=[rt])
            op("act", lambda e: e.activation(out=ra[:, 0:n], in_=ra[:, 0:n], func=AF.Exp, scale=-0.5), reads=[rt], writes=[rt])

    def store_tile_b(ti):
        t0, n = TT512[ti]
        par = sstate["par"]
        st = sstate["st"]
        rot = sstate["rot"]
        if last:
            ra, rt = sstate["rs"][ti % 2]
        for b0 in range(0, n, 128):
            nb = min(128, n - b0)
            tb = t0 + b0
            bi = sstate["bi"]
            sstate["bi"] += 1
            sa, stl = st[bi % 2]
            if last:
                ya, yt = sstate["Y"][bi % 2]
                for kc in range(8):
                    op("dve", lambda e: e.scalar_tensor_tensor(out=ya[:, kc, 0:nb], in0=H[:, kc, tb:tb + nb], scalar=PPt[par][:, 52 + kc:53 + kc],
                                                               in1=ra[:, b0:b0 + nb], op0=ALU.mult, op1=ALU.mult),
                       reads=tl(tH, kc, tb, nb) + [rt, tPar[par]], writes=[yt])
            for g in range(2):
                pso, tpso = rot.next()
                pso3 = pso.rearrange("p (a b) -> p a b", b=128)
                for j in range(4):
                    kc = 4 * g + j
                    if last:
                        src, srt = ya[:, kc, 0:nb], [yt]
                    else:
                        src, srt = H[:, kc, tb:tb + nb], tl(tH, kc, tb, nb)
                    op("pe", lambda e: e.transpose(pso3[0:nb, j, :], src, id_f), reads=srt + [tC], writes=[tpso], mark=(j == 3))
                dst = sa[0:nb, 512 * g:512 * (g + 1)].rearrange("p (a b) -> p a b", b=128)
                if g == 0:
                    op("act", lambda e: e.activation(out=dst, in_=pso3[0:nb, :, :], func=AF.Copy), reads=[tpso], writes=[stl])
                else:
                    op("dve", lambda e: e.tensor_copy(out=dst, in_=pso3[0:nb, :, :]), reads=[tpso], writes=[stl])
            if last:
                if tb == 0:
                    dma("sp", lambda e: e.dma_start(out=hout[0:nb - NMETA, :], in_=sa[NMETA:nb, :]), reads=[stl])
                else:
                    dma("sp", lambda e: e.dma_start(out=hout[tb - NMETA:tb - NMETA + nb, :], in_=sa[0:nb, :]), reads=[stl])
            else:
                dma("sp", lambda e: e.dma_start(out=hout[tb:tb + nb, :], in_=sa[0:nb, :]), reads=[stl])

    def phase_store(par):
        store_setup(par, True)
        for ti in range(len(TT512)):
            store_tile_a(ti)
            store_tile_b(ti)

    def dump(name, buf, tiles, nch):
        if not dbg:
            return
        tk.barrier()
        rt = []
        for c in range(nch):
            rt += tiles[c]
        dma("sp", lambda e: e.dma_start(out=dbg_out[name], in_=buf), reads=rt)

    load_params(0)
    wadvance(0)
    stored = False
    phase_load(with_norm=not dbg)
    for l in range(nl):
        par = l % 2
        if l == 0 and upto >= 1 and dbg:
            phase_norm(par, 36)
        if l == 0 and upto >= 1:
            dump("d_zt", ZT, tZ, 8)
        if upto >= 2:
            phase_rec(l, par)
        if l == 0 and upto >= 2:
            dump("d_mr", MR, tM, 4)
        if upto >= 3:
            phase_attn(l, par)
        if l == 0 and upto >= 3:
            dump("d_qk", QK, tQ, 8)
            dump("d_ma", ZT, tZ, 8)
        if upto >= 4:
            phase_out(l, par, with_norm=(upto >= 5) and not dbg)
        if l == 0:
            dump("d_h1", H, tH, 8)
        if upto >= 5 and dbg:
            phase_norm(par, 44)
        if l + 1 < nl:
            load_params(l + 1)
        if upto >= 6:
            if l + 1 < nl:
                rotn = Rot([0, 1])
                ndone1 = [0]
                phase_mlp(l, after_tile=lambda ti, p2=(l + 1) % 2, nd=ndone1: [lambda: norm_range(TT512[ti][0], TT512[ti][1], p2, 36, rotn, nd)])
            elif dbg:
                phase_mlp(l)
            else:
                phase_mlp(l, after_tile=lambda ti: [lambda: store_tile_a(ti), lambda: store_tile_b(ti)],
                          setup=lambda p2=par: store_setup(p2, False))
                stored = True
        if l == 0:
            dump("d_h2", H, tH, 8)
    if not stored:
        phase_store((nl - 1) % 2)
    tk.barrier()
    return nc


def _pp_layout(inp, l):
    pp = np.zeros((128, NPP), np.float32)
    cw = inp["conv_w"][l]
    for c in range(4):
        for k in range(4):
            pp[:, c * 4 + k] = cw[k, c * 128:(c + 1) * 128]
        pp[:, 16 + c] = inp["conv_b"][l, c * 128:(c + 1) * 128]
        pp[:, 20 + c] = inp["b_gate_a"][l, c * 128:(c + 1) * 128]
        pp[:, 24 + c] = inp["b_gate_x"][l, c * 128:(c + 1) * 128]
        pp[:, 28 + c] = inp["lru_L"][l, c * 128:(c + 1) * 128]
        pp[:, 32 + c] = inp["rec_out_g"][l, c * 128:(c + 1) * 128]
    for kc in range(8):
        pp[:, 36 + kc] = inp["attn_norm_g"][l, kc * 128:(kc + 1) * 128]
        pp[:, 44 + kc] = inp["mlp_norm_g"][l, kc * 128:(kc + 1) * 128]
        pp[:, 52 + kc] = inp["final_g"][kc * 128:(kc + 1) * 128]
    return pp


_NC_CACHE = {}


def _get_nc(nl, last, dbg=False):
    key = (nl, last, dbg)
    if key not in _NC_CACHE:
        _NC_CACHE[key] = build(nl, last, dbg)
    return _NC_CACHE[key]


def run_layers(inp, hin_list, l0, l1, last, dbg=False, ncores=8):
    f = lambda a: np.ascontiguousarray(np.asarray(a, dtype=np.float32))
    nl = l1 - l0
    nc = _get_nc(nl, last, dbg)
    pp = np.stack([_pp_layout(inp, l) for l in range(l0, l1)])
    shared = {
        "w_in": f(inp["w_in"][l0:l1]), "w_out": f(inp["w_out"][l0:l1]), "w_up": f(inp["w_up"][l0:l1]),
        "w_dn": f(inp["w_down"][l0:l1]), "w_ga": f(inp["w_gate_a"][l0:l1]), "w_gx": f(inp["w_gate_x"][l0:l1]),
        "pp": f(pp), "b_f": f(inp["b_f"][l0:l1]), "g_a": f(inp["attn_out_g"][l0:l1]),
    }
    in_maps = [dict(shared, hin=f(hin_list[b])) for b in range(ncores)]
    res = run_bass_kernel_spmd(nc, in_maps, core_ids=list(range(ncores)))
    return res


def kernel(**inputs):
    inp = {k: np.asarray(v) for k, v in inputs.items()}
    x = inp["x"].astype(np.float32)
    meta = inp["meta"].astype(np.float32)
    B = x.shape[0]
    hs = [np.concatenate([meta, x[b]], axis=0) for b in range(B)]
    l0 = 0
    while l0 < DEPTH:
        l1 = min(DEPTH, l0 + NL_PER_LAUNCH)
        last = l1 == DEPTH
        res = run_layers(inp, hs, l0, l1, last)
        hs = [res.results[b]["hout"] for b in range(B)]
        l0 = l1
    return np.stack(hs, axis=0).astype(np.float32)
```

```python
import numpy as np
import concourse.bass as bass
import concourse.mybir as mybir
from concourse.bass_utils import run_bass_kernel_spmd

F32 = mybir.dt.float32
BF16 = mybir.dt.bfloat16
AF = mybir.ActivationFunctionType
ALU = mybir.AluOpType

DEPTH = 4
NL_PER_LAUNCH = 4
T = 2064
S = 2048
D = 1024
NMETA = 16
DIN = 2568
DFF = 4096
EPS = 1e-6
NPP = 60
TT512 = [(0, 512), (512, 512), (1024, 512), (1536, 512), (2048, 16)]
TT256 = [(256 * j, 256) for j in range(8)] + [(2048, 16)]
KT = [(128 * i, 128) for i in range(16)] + [(2048, 16)]
NWB = 5
WCOLS = 256


class Tile:
    __slots__ = ("name", "last_w", "readers", "excl")

    def __init__(self, name, excl=False):
        self.name = name
        self.last_w = None
        self.readers = {}
        self.excl = excl


class Trk:
    def __init__(self, nc, ndma=16):
        self.nc = nc
        self.eng = {"pe": nc.tensor, "act": nc.scalar, "dve": nc.vector, "pool": nc.gpsimd, "sp": nc.sync}
        self.sem = {k: nc.alloc_semaphore("s_" + k) for k in self.eng}
        self.cnt = {k: 0 for k in self.eng}
        self.dsem = [nc.alloc_semaphore("d_%d" % i) for i in range(ndma)]
        self.dcnt = [0] * ndma
        self.dpool = {"pool": list(range(0, ndma // 2)), "sp": list(range(ndma // 2, ndma))}
        self.dnext = {"pool": 0, "sp": 0}
        self.waited = {}
        self.nwaits = 0
        self.pend = {k: False for k in self.eng}

    def _wait(self, e, ev):
        key = (e, ev[0])
        if self.waited.get(key, 0) >= ev[1]:
            return
        self.waited[key] = ev[1]
        s = self.sem[ev[0]] if isinstance(ev[0], str) else self.dsem[ev[0]]
        self.eng[e].wait_ge(s, ev[1])
        self.nwaits += 1

    def _deps(self, e, reads, writes):
        deps = {}

        def add(k, v):
            if deps.get(k, 0) < v:
                deps[k] = v

        for t in reads:
            if t.last_w is not None:
                add(*t.last_w)
            if t.excl:
                for k, v in t.readers.items():
                    if k != e:
                        add(k, v)
        for t in writes:
            if t.last_w is not None:
                add(*t.last_w)
            for k, v in t.readers.items():
                add(k, v)
        for k, v in deps.items():
            if k == "pe" and e == "pe":
                continue
            self._wait(e, (k, v))

    def _upd(self, ev, reads, writes):
        for t in reads:
            if t.readers.get(ev[0], 0) < ev[1]:
                t.readers[ev[0]] = ev[1]
        for t in writes:
            t.last_w = ev
            t.readers = {}

    def op(self, e, fn, reads=(), writes=(), mark=True):
        self._deps(e, reads, writes)
        inst = fn(self.eng[e])
        if mark:
            self.cnt[e] += 1
            inst.then_inc(self.sem[e], 1)
            ev = (e, self.cnt[e])
            self.pend[e] = False
        else:
            ev = (e, self.cnt[e] + 1)
            self.pend[e] = True
        self._upd(ev, reads, writes)
        return inst

    def dma(self, e, fn, reads=(), writes=()):
        self._deps(e, reads, writes)
        pl = self.dpool[e]
        k = pl[self.dnext[e] % len(pl)]
        self.dnext[e] += 1
        if self.dcnt[k] > 0:
            self._wait(e, (k, self.dcnt[k]))
        inst = fn(self.eng[e])
        self.dcnt[k] += 16
        inst.then_inc(self.dsem[k], 16)
        self._upd((k, self.dcnt[k]), reads, writes)
        return inst

    def snapshot(self):
        snap = {}
        for e in self.eng:
            v = self.cnt[e] + (1 if self.pend[e] else 0)
            if v > 0:
                snap[e] = v
        for k in range(len(self.dsem)):
            if self.dcnt[k] > 0:
                snap[k] = self.dcnt[k]
        return snap

    def barrier(self):
        for e in self.eng:
            for e2 in self.eng:
                if e2 != e and self.cnt[e2] > 0:
                    self._wait(e, (e2, self.cnt[e2]))
            for k in range(len(self.dsem)):
                if self.dcnt[k] > 0:
                    self._wait(e, (k, self.dcnt[k]))


class Arena:
    def __init__(self, ap_f32, nbytes):
        self.base = ap_f32
        self.nbytes = nbytes
        self.off = 0
        self.snap = {}

    def reset(self, tk=None):
        self.off = 0
        self.snap = tk.snapshot() if tk is not None else {}

    def take(self, shape, dtype, name):
        esz = 4 if dtype == F32 else 2
        n = 1
        for s in shape[1:]:
            n *= s
        nb = (n * esz + 31) // 32 * 32
        assert self.off + nb <= self.nbytes, (name, self.off, nb, self.nbytes)
        a = self.base[:, self.off // 4:(self.off + nb) // 4]
        self.off += nb
        if dtype != F32:
            a = a.bitcast(dtype)
        a = a[:, 0:n]
        if len(shape) == 3:
            a = a.rearrange("p (a b) -> p a b", b=shape[2])
        t = Tile(name)
        t.readers = dict(self.snap)
        return a, t


def build(nl, last, dbg=False, upto=99):
    nc = bass.Bass("TRN2", target_bir_lowering=False)
    hin = nc.dram_tensor("hin", [T, D], F32, kind="ExternalInput").ap()
    w_in = nc.dram_tensor("w_in", [nl, D, DIN], F32, kind="ExternalInput").ap()
    w_out = nc.dram_tensor("w_out", [nl, D, D], F32, kind="ExternalInput").ap()
    w_up = nc.dram_tensor("w_up", [nl, D, DFF], F32, kind="ExternalInput").ap()
    w_dn = nc.dram_tensor("w_dn", [nl, DFF, D], F32, kind="ExternalInput").ap()
    w_ga = nc.dram_tensor("w_ga", [nl, 8, 64, 64], F32, kind="ExternalInput").ap()
    w_gx = nc.dram_tensor("w_gx", [nl, 8, 64, 64], F32, kind="ExternalInput").ap()
    ppd = nc.dram_tensor("pp", [nl, 128, NPP], F32, kind="ExternalInput").ap()
    bfd = nc.dram_tensor("b_f", [nl, 8], F32, kind="ExternalInput").ap()
    gad = nc.dram_tensor("g_a", [nl, 512], F32, kind="ExternalInput").ap()
    nrows_out = S if last else T
    hout = nc.dram_tensor("hout", [nrows_out, D], F32, kind="ExternalOutput").ap()
    dbg_out = {}
    if dbg:
        dbg_out["d_zt"] = nc.dram_tensor("d_zt", [128, 8, T], BF16, kind="ExternalOutput").ap()
        dbg_out["d_mr"] = nc.dram_tensor("d_mr", [128, 4, T], BF16, kind="ExternalOutput").ap()
        dbg_out["d_qk"] = nc.dram_tensor("d_qk", [128, 8, T], BF16, kind="ExternalOutput").ap()
        dbg_out["d_ma"] = nc.dram_tensor("d_ma", [128, 8, T], BF16, kind="ExternalOutput").ap()
        dbg_out["d_h1"] = nc.dram_tensor("d_h1", [128, 8, T], F32, kind="ExternalOutput").ap()
        dbg_out["d_h2"] = nc.dram_tensor("d_h2", [128, 8, T], F32, kind="ExternalOutput").ap()

    tk = Trk(nc)
    op, dma = tk.op, tk.dma

    def sb(name, shape, dt):
        return nc.alloc_sbuf_tensor(name, shape, dt).ap()

    H = sb("H", [128, 8, T], F32)
    ZT = sb("ZT", [128, 8, T], BF16)
    QK = sb("QK", [128, 8, T], BF16)
    MR = sb("MR", [128, 4, T], BF16)
    tH = [[Tile("H%d_%d" % (c, j)) for j in range(9)] for c in range(8)]
    tZ = [[Tile("Z%d_%d" % (c, j)) for j in range(9)] for c in range(8)]
    tQ = [[Tile("Q%d_%d" % (c, j)) for j in range(9)] for c in range(8)]
    tM = [[Tile("M%d_%d" % (c, j)) for j in range(9)] for c in range(4)]

    def tl(tiles, c, t0, n):
        j0 = t0 // 256
        j1 = (t0 + n - 1) // 256
        return [tiles[c][j] for j in range(j0, j1 + 1)]

    def tls(tiles, cs, t0, n):
        r = []
        for c in cs:
            r += tl(tiles, c, t0, n)
        return r

    WB = [sb("WB%d" % i, [128, 8, WCOLS], BF16) for i in range(NWB)]
    tWB = [Tile("WB%d" % i) for i in range(NWB)]
    ones_bf = sb("ones_bf", [128, 128], BF16)
    ones_f = sb("ones_f", [128, 128], F32)
    tri_f = sb("tri_f", [128, 128], F32)
    tri_bf = sb("tri_bf", [128, 128], BF16)
    id_f = sb("id_f", [128, 128], F32)
    id_bf = sb("id_bf", [128, 128], BF16)
    EPSB = sb("EPSB", [128, 1], F32)
    tC = Tile("consts")
    PPt = [sb("PP%d" % i, [128, NPP], F32) for i in range(2)]
    BFt = [sb("BF%d" % i, [128, 8], F32) for i in range(2)]
    GAt1 = sb("GAg", [128, 512], F32)
    GAt = [GAt1, GAt1]
    tGAt = Tile("GAt")
    GA = [sb("GA%d" % i, [128, 4, 128], BF16) for i in range(2)]
    GX = [sb("GX%d" % i, [128, 4, 128], BF16) for i in range(2)]
    WF = [sb("WF%d" % i, [128, 8, 8], BF16) for i in range(2)]
    tPar = [Tile("par%d" % i) for i in range(2)]
    SM = sb("SM", [128, 64], F32)
    tSM = Tile("SM")
    CARRY = sb("CARRY", [128, 4, 3], F32)
    HL = sb("HL", [128, 4], F32)
    tCar = [Tile("car%d" % c) for c in range(4)]
    tHL = [Tile("hl%d" % c) for c in range(4)]
    NSQ = [(sb("nsq%d" % i, [128, 256], BF16), Tile("nsq%d" % i)) for i in range(2)]
    NRS = [(sb("nrs%d" % i, [128, 256], F32), Tile("nrs%d" % i)) for i in range(1)]
    nstate = {"k": 0, "r": 0}
    ARENA_BYTES = (nc.sbuf_bytes_remaining - 64) // 32 * 32
    print("ARENA_BYTES", ARENA_BYTES)
    AR = Arena(sb("ARENA", [128, ARENA_BYTES // 4], F32), ARENA_BYTES)

    PS = [nc.alloc_psum_tensor("ps%d" % i, [128, 512], F32).ap() for i in range(8)]
    tPS = [Tile("ps%d" % i, excl=True) for i in range(8)]

    class Rot:
        def __init__(self, idxs):
            self.idxs = idxs
            self.k = 0

        def next(self):
            i = self.idxs[self.k % len(self.idxs)]
            self.k += 1
            return PS[i], tPS[i]

    op("pool", lambda e: e.memset(ones_bf, 1.0), writes=[tC])
    op("pool", lambda e: e.memset(ones_f, 1.0), writes=[tC])
    op("pool", lambda e: e.memset(EPSB, EPS), writes=[tC])
    for a in (tri_f, tri_bf):
        op("pool", lambda e: e.memset(a, 1.0), writes=[tC])
        op("pool", lambda e: e.affine_select(out=a, in_=a, pattern=[[1, 128]], compare_op=ALU.is_ge, fill=0.0,
                                             base=0, channel_multiplier=-1), reads=[tC], writes=[tC])
    for a in (id_f, id_bf):
        op("pool", lambda e: e.memset(a, 1.0), writes=[tC])
        op("pool", lambda e: e.affine_select(out=a, in_=a, pattern=[[1, 128]], compare_op=ALU.is_equal, fill=0.0,
                                             base=0, channel_multiplier=-1), reads=[tC], writes=[tC])
    for i in range(2):
        op("pool", lambda e: e.memset(GA[i], 0.0), writes=[tPar[i]])
        op("pool", lambda e: e.memset(GX[i], 0.0), writes=[tPar[i]])

    units = []

    def wview(wt, l, r0, c0):
        return wt[l, r0:r0 + 1024, c0:c0 + WCOLS].rearrange("(kc p) c -> p kc c", p=128)

    U = {}
    for l in range(nl):
        for name, base in (("xr", 1544), ("yr", 2056), ("q", 0), ("v", 1024), ("k", 512)):
            for u in range(2):
                U[(l, name, u)] = len(units)
                units.append(wview(w_in, l, 0, base + u * WCOLS))
        for u in range(4):
            U[(l, "o", u)] = len(units)
            units.append(wview(w_out, l, 0, u * WCOLS))
        for g in range(4):
            for u in range(4):
                U[(l, "up", g, u)] = len(units)
                units.append(wview(w_up, l, 0, g * 1024 + u * WCOLS))
            for u in range(4):
                U[(l, "dn", g, u)] = len(units)
                units.append(wview(w_dn, l, g * 1024, u * WCOLS))
    wstate = {"issued": 0}

    def wadvance(oldest):
        while wstate["issued"] < min(len(units), oldest + NWB):
            j = wstate["issued"]
            b = j % NWB
            src = units[j]
            dma("pool", lambda e: e.dma_start(out=WB[b], in_=src), writes=[tWB[b]])
            wstate["issued"] += 1

    def wunit(j):
        assert j < wstate["issued"] and j >= wstate["issued"] - NWB, (j, wstate["issued"])
        return WB[j % NWB], tWB[j % NWB]

    def load_params(l):
        i = l % 2
        dma("sp", lambda e: e.dma_start(out=PPt[i], in_=ppd[l]), writes=[tPar[i]])
        dma("sp", lambda e: e.dma_start(out=BFt[i], in_=bfd[l:l + 1, :].partition_broadcast(128)), writes=[tPar[i]])
        dma("sp", lambda e: e.dma_start(out=GAt[i], in_=gad[l:l + 1, :].partition_broadcast(128)), writes=[tGAt])
        for q in range(2):
            for (dst, srcw) in ((GA[i], w_ga), (GX[i], w_gx)):
                for nn in range(4):
                    dma("pool", lambda e: e.dma_start(out=dst[64 * q:64 * q + 64, nn, 64 * q:64 * q + 64],
                                                      in_=srcw[l, 2 * nn + q]), writes=[tPar[i]])
        dma("pool", lambda e: e.dma_start(
            out=WF[i], in_=w_in[l, :, 1536:1544].rearrange("(kc p) c -> p kc c", p=128)), writes=[tPar[i]])

    import threading
    coop = {"cur": None}

    def cop(*a, **k):
        r = op(*a, **k)
        th = coop["cur"]
        if th is not None:
            coop["main"].set()
            th["evt"].wait()
            th["evt"].clear()
            coop["cur"] = th
        return r

    def cidle(n):
        th = coop["cur"]
        for _ in range(n):
            if th is not None:
                coop["main"].set()
                th["evt"].wait()
                th["evt"].clear()
                coop["cur"] = th

    def coop_run(fns, width, stagger=0, bg=None):
        main_evt = threading.Event()
        coop["main"] = main_evt
        pending = list(fns)
        slots = [None] * (width + (1 if bg is not None else 0))
        errs = []

        def start(fn, slot):
            th = {"evt": threading.Event(), "done": False}

            def body():
                th["evt"].wait()
                th["evt"].clear()
                coop["cur"] = th
                try:
                    fn(slot)
                except BaseException as ex:
                    errs.append(ex)
                finally:
                    th["done"] = True
                    coop["cur"] = None
                    main_evt.set()

            th["t"] = threading.Thread(target=body)
            th["t"].start()
            return th

        rnd = 0
        started = 0
        if bg is not None:
            slots[width] = start(bg, width)
        while pending or any(x is not None for x in slots):
            rnd += 1
            for si in range(len(slots)):
                if si < width and slots[si] is None and pending and (started >= width or rnd > started * stagger):
                    slots[si] = start(pending.pop(0), si)
                    started += 1
                th = slots[si]
                if th is None:
                    continue
                th["evt"].set()
                main_evt.wait()
                main_evt.clear()
                if errs:
                    raise errs[0]
                if th["done"]:
                    th["t"].join()
                    slots[si] = None
        coop["cur"] = None

    def phase_load(with_norm=False):
        rotn = Rot([4, 5])
        AR.reset(tk)
        st = [AR.take([128, 1024], F32, "st%d" % i) for i in range(2)]
        rot = Rot([0, 1, 2, 3])
        for i, (t0, n) in enumerate(KT):
            sa, stl = st[i % 2]
            dma("sp", lambda e: e.dma_start(out=sa[0:n, :], in_=hin[t0:t0 + n, :]), writes=[stl])
            for g in range(2):
                ps, tps = rot.next()
                ps3 = ps.rearrange("p (a b) -> p a b", b=128)
                for j in range(4):
                    kc = 4 * g + j
                    op("pe", lambda e: e.transpose(ps3[:, j, 0:n], sa[0:n, kc * 128:(kc + 1) * 128], id_f[0:n, 0:n]),
                       reads=[stl, tC], writes=[tps], mark=(j == 3))
                eng = "act" if g == 0 else "dve"
                if eng == "act":
                    op("act", lambda e: e.activation(out=H[:, 4 * g:4 * g + 4, t0:t0 + n], in_=ps3[:, :, 0:n], func=AF.Copy),
                       reads=[tps], writes=tls(tH, range(4 * g, 4 * g + 4), t0, n))
                else:
                    op("dve", lambda e: e.tensor_copy(out=H[:, 4 * g:4 * g + 4, t0:t0 + n], in_=ps3[:, :, 0:n]),
                       reads=[tps], writes=tls(tH, range(4 * g, 4 * g + 4), t0, n))
            if with_norm and (i % 2 == 1 or i == 16):
                norm_tile(i // 2, 0, 36, rotn)


    def norm_tile(j, par, gcol, rot):
        t0, n = TT256[j]
        ps, tps = rot.next()
        for kc in range(8):
            sqa, sqt = NSQ[nstate["k"] % 2]
            nstate["k"] += 1
            op("act", lambda e: e.activation(out=sqa[:, 0:n], in_=H[:, kc, t0:t0 + n], func=AF.Square),
               reads=tl(tH, kc, t0, n), writes=[sqt])
            op("pe", lambda e: e.matmul(ps[:, 0:n], lhsT=ones_bf, rhs=sqa[:, 0:n], start=(kc == 0), stop=(kc == 7)),
               reads=[sqt, tC], writes=[tps])
        ra, rt = NRS[0]
        nstate["r"] += 1
        op("act", lambda e: e.activation(out=ra[:, 0:n], in_=ps[:, 0:n], func=AF.Ln, bias=EPSB, scale=1.0 / D), reads=[tps, tC], writes=[rt])
        op("act", lambda e: e.activation(out=ra[:, 0:n], in_=ra[:, 0:n], func=AF.Exp, scale=-0.5), reads=[rt], writes=[rt])
        for kc in range(8):
            op("dve", lambda e: e.scalar_tensor_tensor(out=ZT[:, kc, t0:t0 + n], in0=H[:, kc, t0:t0 + n],
                                                       scalar=PPt[par][:, gcol + kc:gcol + kc + 1], in1=ra[:, 0:n],
                                                       op0=ALU.mult, op1=ALU.mult),
               reads=tl(tH, kc, t0, n) + [rt, tPar[par]], writes=tl(tZ, kc, t0, n))

    def norm_range(t0, n, par, gcol, rot, done):
        while done[0] < len(TT256) and TT256[done[0]][0] + TT256[done[0]][1] <= t0 + n:
            norm_tile(done[0], par, gcol, rot)
            done[0] += 1

    def phase_norm(par, gcol):
        rot = Rot([0, 1])
        for j in range(len(TT256)):
            norm_tile(j, par, gcol, rot)

    def phase_rec(l, par):
        AR.reset(tk)
        NS = 5
        sets = []
        for s in range(NS):
            d = {}
            d["X"] = AR.take([128, 259], F32, "X%d" % s)
            d["C"] = AR.take([128, 256], F32, "C%d" % s)
            d["R"] = AR.take([128, 256], F32, "R%d" % s)
            d["I"] = AR.take([128, 256], F32, "I%d" % s)
            d["A"] = AR.take([128, 256], F32, "A%d" % s)
            sets.append(d)
        REC = [AR.take([128, 4, 256], BF16, "REC%d" % i) for i in range(2)]
        RS = [AR.take([128, 256], F32, "RS0")] * 2
        PP = PPt[par]
        tP = tPar[par]
        op("act", lambda e: e.activation(out=SM[:, 16:20], in_=PP[:, 28:32], func=AF.Exp, scale=-1.0), reads=[tP], writes=[tSM])
        op("act", lambda e: e.activation(out=SM[:, 16:20], in_=SM[:, 16:20], func=AF.Ln, bias=1.0), reads=[tSM], writes=[tSM])
        op("dve", lambda e: e.tensor_scalar(out=SM[:, 0:4], in0=SM[:, 16:20], scalar1=-8.0, scalar2=None, op0=ALU.mult), reads=[tSM], writes=[tSM])
        op("dve", lambda e: e.tensor_scalar(out=SM[:, 4:8], in0=SM[:, 16:20], scalar1=-16.0, scalar2=None, op0=ALU.mult), reads=[tSM], writes=[tSM])
        op("dve", lambda e: e.tensor_scalar(out=SM[:, 8:16], in0=PP[:, 20:28], scalar1=-1.0, scalar2=None, op0=ALU.mult), reads=[tP, tSM], writes=[tSM])
        op("dve", lambda e: e.memset(CARRY, 0.0), writes=tCar)
        u0 = U[(l, "xr", 0)]
        wadvance(u0)
        Wxr = [wunit(u0), wunit(u0 + 1)]
        Wyr = [wunit(u0 + 2), wunit(u0 + 3)]
        rotP = Rot([0, 1, 2])
        rotG = Rot([3, 4, 5])
        psSs = [(PS[6], tPS[6]), (PS[7], tPS[7])]
        RECt = [[Tile("rec%d_%d" % (i, c)) for c in range(4)] for i in range(2)]
        for i_ in range(2):
            for t_ in RECt[i_]:
                t_.readers = dict(AR.snap)

        def chain(n, c0, slot):
            op = cop
            c = c0
            t0, N = TT256[n]
            reca = REC[n % 2][0]
            rect = RECt[n % 2][c]
            rsa, rst = RS[n % 2]
            psS, tpsS = psSs[n % 2]
            d = sets[slot]
            (X, tX), (C, tCc), (R, tR), (I, tI), (A, tA) = (d[k] for k in ("X", "C", "R", "I", "A"))
            CB, tCB = X[:, 0:128].bitcast(BF16), tX
            A2, tA2 = X, tX
            wa, wt = Wxr[c // 2]
            ps, tps = PS[slot], tPS[slot]
            for kc in range(8):
                op("pe", lambda e: e.matmul(ps[:, 0:N], lhsT=wa[:, kc, (c % 2) * 128:(c % 2) * 128 + 128],
                                            rhs=ZT[:, kc, t0:t0 + N], start=(kc == 0), stop=(kc == 7)),
                   reads=[wt] + tl(tZ, kc, t0, N), writes=[tps], mark=(kc == 7))
            op("act", lambda e: e.activation(out=X[:, 3:3 + N], in_=ps[:, 0:N], func=AF.Copy), reads=[tps], writes=[tX])
            op("dve", lambda e: e.tensor_copy(out=X[:, 0:3], in_=CARRY[:, c, :]), reads=[tCar[c]], writes=[tX])
            op("pool", lambda e: e.tensor_copy(out=CARRY[:, c, :], in_=X[:, N:N + 3]), reads=[tX], writes=[tCar[c]])
            op("dve", lambda e: e.tensor_scalar(out=C[:, 0:N], in0=X[:, 3:3 + N], scalar1=PP[:, c * 4 + 3:c * 4 + 4],
                                                scalar2=PP[:, 16 + c:17 + c], op0=ALU.mult, op1=ALU.add),
               reads=[tX, tP], writes=[tCc])
            for kk in (2, 1, 0):
                op("dve", lambda e: e.scalar_tensor_tensor(out=C[:, 0:N], in0=X[:, kk:kk + N],
                                                           scalar=PP[:, c * 4 + kk:c * 4 + kk + 1], in1=C[:, 0:N],
                                                           op0=ALU.mult, op1=ALU.add), reads=[tX, tP, tCc], writes=[tCc])
            op("pool", lambda e: e.tensor_copy(out=CB[:, 0:N], in_=C[:, 0:N]), reads=[tCc], writes=[tCB])
            psg, tpsa = PS[slot], tPS[slot]
            tpsi = tpsa
            psa = psg[:, 0:256]
            psi = psg[:, 256:512]
            op("pe", lambda e: e.matmul(psa[:, 0:N], lhsT=GA[par][:, c, :], rhs=CB[:, 0:N], start=True, stop=True),
               reads=[tCB, tP], writes=[tpsa])
            op("pe", lambda e: e.matmul(psi[:, 0:N], lhsT=GX[par][:, c, :], rhs=CB[:, 0:N], start=True, stop=True),
               reads=[tCB, tP], writes=[tpsi])
            op("act", lambda e: e.activation(out=R[:, 0:N], in_=psa[:, 0:N], func=AF.Exp, bias=SM[:, 8 + c:9 + c], scale=-1.0),
               reads=[tpsa, tSM], writes=[tR])
            op("act", lambda e: e.activation(out=I[:, 0:N], in_=psi[:, 0:N], func=AF.Exp, bias=SM[:, 12 + c:13 + c], scale=-1.0),
               reads=[tpsi, tSM], writes=[tI])
            op("act", lambda e: e.activation(out=R[:, 0:N], in_=R[:, 0:N], func=AF.Ln, bias=1.0), reads=[tR], writes=[tR])
            op("act", lambda e: e.activation(out=R[:, 0:N], in_=R[:, 0:N], func=AF.Exp, scale=-1.0), reads=[tR], writes=[tR])
            op("act", lambda e: e.activation(out=I[:, 0:N], in_=I[:, 0:N], func=AF.Ln, bias=1.0), reads=[tI], writes=[tI])
            op("act", lambda e: e.activation(out=A[:, 0:N], in_=R[:, 0:N], func=AF.Exp, scale=SM[:, c:c + 1]), reads=[tR, tSM], writes=[tA])
            op("act", lambda e: e.activation(out=A2[:, 0:N], in_=R[:, 0:N], func=AF.Exp, scale=SM[:, 4 + c:5 + c]), reads=[tR, tSM], writes=[tA2])
            op("act", lambda e: e.activation(out=A2[:, 0:N], in_=A2[:, 0:N], func=AF.Ln, bias=1.0, scale=-1.0), reads=[tA2], writes=[tA2])
            op("dve", lambda e: e.scalar_tensor_tensor(out=I[:, 0:N], in0=A2[:, 0:N], scalar=0.5, in1=I[:, 0:N], op0=ALU.mult, op1=ALU.subtract),
               reads=[tI, tA2], writes=[tI])
            op("act", lambda e: e.activation(out=I[:, 0:N], in_=I[:, 0:N], func=AF.Exp), reads=[tI], writes=[tI])
            op("dve", lambda e: e.tensor_tensor(out=I[:, 0:N], in0=I[:, 0:N], in1=C[:, 0:N], op=ALU.mult), reads=[tI, tCc], writes=[tI])
            init = 0.0 if n == 0 else HL[:, c:c + 1]
            op("dve", lambda e: e.tensor_tensor_scan(out=C[:, 0:N], data0=A[:, 0:N], data1=I[:, 0:N], initial=init, op0=ALU.mult, op1=ALU.add),
               reads=[tA, tI, tHL[c]], writes=[tCc])
            op("pool", lambda e: e.tensor_copy(out=HL[:, c:c + 1], in_=C[:, N - 1:N]), reads=[tCc], writes=[tHL[c]])
            wa, wt = Wyr[c // 2]
            ps, tps = PS[slot], tPS[slot]
            for kc in range(8):
                op("pe", lambda e: e.matmul(ps[:, 0:N], lhsT=wa[:, kc, (c % 2) * 128:(c % 2) * 128 + 128],
                                            rhs=ZT[:, kc, t0:t0 + N], start=(kc == 0), stop=(kc == 7)),
                   reads=[wt] + tl(tZ, kc, t0, N), writes=[tps], mark=(kc == 7))
            op("act", lambda e: e.activation(out=R[:, 0:N], in_=ps[:, 0:N], func=AF.Copy), reads=[tps], writes=[tR])
            op("dve", lambda e: e.tensor_tensor(out=X[:, 0:N], in0=R[:, 0:N], in1=R[:, 0:N], op=ALU.mult), reads=[tR], writes=[tX])
            op("dve", lambda e: e.tensor_scalar(out=X[:, 0:N], in0=X[:, 0:N], scalar1=0.044715, scalar2=1.0, op0=ALU.mult, op1=ALU.add),
               reads=[tX], writes=[tX])
            op("dve", lambda e: e.tensor_tensor(out=X[:, 0:N], in0=X[:, 0:N], in1=R[:, 0:N], op=ALU.mult), reads=[tX, tR], writes=[tX])
            op("act", lambda e: e.activation(out=X[:, 0:N], in_=X[:, 0:N], func=AF.Exp, scale=-2.0 * 0.7978845608028654), reads=[tX], writes=[tX])
            op("act", lambda e: e.activation(out=X[:, 0:N], in_=X[:, 0:N], func=AF.Ln, bias=1.0), reads=[tX], writes=[tX])
            op("act", lambda e: e.activation(out=X[:, 0:N], in_=X[:, 0:N], func=AF.Exp, scale=-1.0), reads=[tX], writes=[tX])
            op("dve", lambda e: e.tensor_tensor(out=R[:, 0:N], in0=R[:, 0:N], in1=C[:, 0:N], op=ALU.mult), reads=[tR, tCc], writes=[tR])
            op("dve", lambda e: e.tensor_tensor(out=reca[:, c, 0:N], in0=R[:, 0:N], in1=X[:, 0:N], op=ALU.mult), reads=[tR, tX], writes=[rect])
            op("act", lambda e: e.activation(out=CB[:, 0:N], in_=reca[:, c, 0:N], func=AF.Square), reads=[rect], writes=[tCB])
            op("pe", lambda e: e.matmul(psS[:, 0:N], lhsT=ones_bf, rhs=CB[:, 0:N], start=(c == 0), stop=(c == 3)),
               reads=[tCB, tC], writes=[tpsS])
            if c0 == 3:
                op("act", lambda e: e.activation(out=rsa[:, 0:N], in_=psS[:, 0:N], func=AF.Ln, bias=EPSB, scale=1.0 / 512), reads=[tpsS, tC], writes=[rst])
                op("act", lambda e: e.activation(out=rsa[:, 0:N], in_=rsa[:, 0:N], func=AF.Exp, scale=-0.5), reads=[rst], writes=[rst])
                for c in range(4):
                    op("dve", lambda e: e.scalar_tensor_tensor(out=MR[:, c, t0:t0 + N], in0=reca[:, c, 0:N], scalar=PP[:, 32 + c:33 + c],
                                                               in1=rsa[:, 0:N], op0=ALU.mult, op1=ALU.mult),
                       reads=[RECt[n % 2][c], rst, tP], writes=tl(tM, c, t0, N))

        def bg_q0(slot):
            wa, wt = wunit(U[(l, "q", 0)])
            rotb = Rot([5])
            for mm in range(2):
                for (t0, n) in TT512:
                    ps, tps = rotb.next()
                    for kc in range(8):
                        cop("pe", lambda e: e.matmul(ps[:, 0:n], lhsT=wa[:, kc, mm * 128:mm * 128 + 128], rhs=ZT[:, kc, t0:t0 + n],
                                                     start=(kc == 0), stop=(kc == 7)),
                            reads=[wt] + tl(tZ, kc, t0, n), writes=[tps], mark=(kc == 7))
                        cidle(3)
                    if mm % 2 == 0:
                        cop("act", lambda e: e.activation(out=QK[:, mm, t0:t0 + n], in_=ps[:, 0:n], func=AF.Copy),
                            reads=[tps], writes=tl(tQ, mm, t0, n))
                    else:
                        cop("dve", lambda e: e.tensor_copy(out=QK[:, mm, t0:t0 + n], in_=ps[:, 0:n]),
                            reads=[tps], writes=tl(tQ, mm, t0, n))
                    cidle(3)

        coop_run([(lambda slot, n=n, c=c: chain(n, c, slot)) for n in range(len(TT256)) for c in range(4)], NS, stagger=10, bg=bg_q0)
        qdone[l] = True

    def phase_attn(l, par):
        AR.reset(tk)
        VP, _ = AR.take([128, 17, 520], BF16, "VP")
        tV = [Tile("V%d" % i) for i in range(17)]
        for t_ in tV:
            t_.readers = dict(AR.snap)
        Lt, tL = AR.take([128, 17, 8], F32, "L")
        Lp, tLp = AR.take([128, 18, 8], F32, "Lp")
        CT, tCT = AR.take([128, 35, 8], F32, "CT")
        boff = [sum((2 * jj + 2) for jj in range(J_)) for J_ in range(9)]
        BT, tBT = AR.take([128, 89, 8], F32, "BT")
        tBTs = [Tile("BT%d" % j_) for j_ in range(9)]
        for t_ in tBTs:
            t_.readers = dict(AR.snap)
        PT = [AR.take([128, 256], BF16, "PT%d" % i) for i in range(3)]
        AT = [AR.take([128, 512], F32, "AT%d" % i) for i in range(2)]
        AN = [AR.take([128, 512], BF16, "AN%d" % i) for i in range(2)]
        RR, tRR = AR.take([128, 8], F32, "RR")
        SSq, tSS = AR.take([128, 4], F32, "SSq")
        PP = PPt[par]
        tP = tPar[par]
        rot = Rot([0, 1, 2, 3])
        evs = {"ev": 0}

        def proj_qk(which, cbase):
            ev = evs["ev"]
            for u in range(2):
                j = U[(l, which, u)]
                wadvance(j)
                if which == "q" and u == 0 and qdone.get(l):
                    continue
                wa, wt = wunit(j)
                for mm in range(2):
                    m = 2 * u + mm
                    for (t0, n) in TT512:
                        ps, tps = rot.next()
                        for kc in range(8):
                            op("pe", lambda e: e.matmul(ps[:, 0:n], lhsT=wa[:, kc, mm * 128:mm * 128 + 128], rhs=ZT[:, kc, t0:t0 + n],
                                                        start=(kc == 0), stop=(kc == 7)),
                               reads=[wt] + tl(tZ, kc, t0, n), writes=[tps], mark=(kc == 7))
                        if m % 2 == 0:
                            op("act", lambda e: e.activation(out=QK[:, cbase + m, t0:t0 + n], in_=ps[:, 0:n], func=AF.Copy),
                               reads=[tps], writes=tl(tQ, cbase + m, t0, n))
                        else:
                            op("dve", lambda e: e.tensor_copy(out=QK[:, cbase + m, t0:t0 + n], in_=ps[:, 0:n]),
                               reads=[tps], writes=tl(tQ, cbase + m, t0, n))
                        ev += 1
            evs["ev"] = ev

        proj_qk("q", 0)
        jv = U[(l, "v", 0)]
        wadvance(jv)
        Wv = [wunit(jv), wunit(jv + 1)]
        VPo = VP.rearrange("p i (h e) -> p i h e", e=65)
        op("pool", lambda e: e.memset(VPo[:, :, :, 64:65], 1.0), writes=tV)
        op("pool", lambda e: e.memset(Lt, 0.0), writes=[tL])
        op("pool", lambda e: e.memset(Lp[:, 0, :], 0.0), writes=[tLp])
        psF, tpsF = PS[7], tPS[7]
        rotv = Rot([4, 5, 6])
        for i, (t0, nk) in enumerate(KT):
            for u in range(2):
                wa, wt = Wv[u]
                ps, tps = rotv.next()
                for kc in range(8):
                    op("pe", lambda e: e.matmul(ps[0:nk, 0:256], lhsT=ZT[:, kc, t0:t0 + nk], rhs=wa[:, kc, :], start=(kc == 0), stop=(kc == 7)),
                       reads=[wt] + tl(tZ, kc, t0, nk), writes=[tps], mark=(kc == 7))
                dst = VP[0:nk, i, u * 260:(u + 1) * 260].rearrange("p (h e) -> p h e", e=65)[:, :, 0:64]
                src = ps[0:nk, 0:256].rearrange("p (h d) -> p h d", d=64)
                if i % 2 == 0:
                    op("act", lambda e: e.activation(out=dst, in_=src, func=AF.Copy), reads=[tps], writes=[tV[i]])
                else:
                    op("dve", lambda e: e.tensor_copy(out=dst, in_=src), reads=[tps], writes=[tV[i]])
            for kc in range(8):
                op("pe", lambda e: e.matmul(psF[0:nk, i * 8:i * 8 + 8], lhsT=ZT[:, kc, t0:t0 + nk], rhs=WF[par][:, kc, :], start=(kc == 0), stop=(kc == 7)),
                   reads=[tP] + tl(tZ, kc, t0, nk), writes=[tpsF], mark=(kc == 7))
            op("dve", lambda e: e.tensor_tensor(out=Lt[0:nk, i, :], in0=psF[0:nk, i * 8:i * 8 + 8], in1=BFt[par][0:nk, :], op=ALU.add),
               reads=[tpsF, tP], writes=[tL])
        for (p1, i0, i1) in ((128, 0, 16), (16, 16, 17)):
            op("act", lambda e: e.activation(out=Lt[0:p1, i0:i1, :], in_=Lt[0:p1, i0:i1, :], func=AF.Exp, scale=-1.0), reads=[tL], writes=[tL])
            op("act", lambda e: e.activation(out=Lt[0:p1, i0:i1, :], in_=Lt[0:p1, i0:i1, :], func=AF.Ln, bias=1.0), reads=[tL], writes=[tL])
        for i in range(17):
            op("dve", lambda e: e.tensor_tensor(out=Lp[:, i + 1, :], in0=Lp[:, i, :], in1=Lt[:, i, :], op=ALU.add), reads=[tL, tLp], writes=[tLp])
        proj_qk("k", 4)
        psC, tpsC = PS[6], tPS[6]
        Ltf = Lt.rearrange("p i h -> p (i h)")
        Lpf = Lp.rearrange("p i h -> p (i h)")
        op("pe", lambda e: e.matmul(psC[:, 0:136], lhsT=tri_f, rhs=Ltf, start=True, stop=False), reads=[tL, tC], writes=[tpsC], mark=False)
        op("pe", lambda e: e.matmul(psC[:, 0:136], lhsT=ones_f, rhs=Lpf[:, 0:136], start=False, stop=True), reads=[tLp, tC], writes=[tpsC], mark=False)
        op("pe", lambda e: e.matmul(psC[:, 136:280], lhsT=ones_f, rhs=Lpf[:, 0:144], start=True, stop=True), reads=[tLp, tC], writes=[tpsC])
        CTf = CT.rearrange("p i h -> p (i h)")
        op("dve", lambda e: e.tensor_copy(out=CTf, in_=psC[:, 0:280]), reads=[tpsC], writes=[tCT])
        for J in range(9):
            ni = 2 * J + 2 if J < 8 else 17
            rJ = 17 + (2 * J + 1 if J < 8 else 17)
            for h in range(8):
                op("dve", lambda e: e.tensor_scalar(out=BT[:, boff[J]:boff[J] + ni, h], in0=CT[:, 0:ni, h], scalar1=CT[:, rJ, h:h + 1],
                                                    scalar2=None, op0=ALU.subtract), reads=[tCT], writes=[tBTs[J]])
        rotS = Rot([0, 1, 2])
        psT, tpsT = PS[7], tPS[7]
        psT_bf = psT.bitcast(BF16).rearrange("p (c t) -> p c t", t=128)
        osets = [[(PS[3], tPS[3]), (PS[4], tPS[4])], [(PS[5], tPS[5]), (PS[6], tPS[6])]]
        items = []
        hcount = 0
        for J in range(9):
            imax = 2 * J + 1 if J < 8 else 16
            for h in range(8):
                for i in range(imax + 1):
                    items.append((J, h, i, imax, hcount))
                hcount += 1
        LA = 2
        state = {}
        deferred = []

        def emit_S(t):
            J, h, i, imax, hc = items[t]
            hp, po = h // 2, 64 * (h % 2)
            q0 = 256 * J
            k0, nk = KT[i]
            if J < 8:
                blks = [0, 1] if i < imax else [1]
                nqb = 128
            else:
                blks = [0]
                nqb = 16
            qs = q0 + 128 * blks[0]
            nq = nqb * len(blks)
            psS, tpsS = rotS.next()
            op("pe", lambda e: e.matmul(psS[0:nk, 0:nq], lhsT=QK[po:po + 64, 4 + hp, k0:k0 + nk], rhs=QK[po:po + 64, hp, qs:qs + nq],
                                        start=True, stop=True),
               reads=tl(tQ, 4 + hp, k0, nk) + tl(tQ, hp, qs, nq), writes=[tpsS])
            state[t] = (psS, tpsS, blks, nqb, nq, nk)

        def epilogue_pe(J, b, nqb, tq0):
            ana, ant = AN[b]
            for c in range(4):
                op("pe", lambda e: e.transpose(psT_bf[:, c, 0:nqb], ana[0:nqb, c * 128:(c + 1) * 128], id_bf[0:nqb, 0:nqb]),
                   reads=[ant, tC], writes=[tpsT], mark=(c == 3))
            op("dve", lambda e: e.tensor_copy(out=ZT[:, 0:4, tq0:tq0 + nqb], in_=psT_bf[:, 0:4, 0:nqb]),
               reads=[tpsT], writes=tls(tZ, range(4), tq0, nqb))

        def emit_rest(t):
            J, h, i, imax, hc = items[t]
            psS, tpsS, blks, nqb, nq, nk = state.pop(t)
            oset = osets[hc % 2]
            pta, ptt = PT[t % 3]
            op("act", lambda e: e.activation(out=pta[0:nk, 0:nq], in_=psS[0:nk, 0:nq], func=AF.Exp,
                                             bias=BT[0:nk, boff[J] + i, h:h + 1], scale=0.125),
               reads=[tpsS, tBTs[J]], writes=[ptt])
            diag = (J < 8 and i >= 2 * J) or (J == 8 and i == 16)
            if diag:
                op("dve", lambda e: e.tensor_tensor(out=pta[0:nk, 0:nqb], in0=pta[0:nk, 0:nqb], in1=tri_bf[0:nk, 0:nqb], op=ALU.mult),
                   reads=[ptt, tC], writes=[ptt])
            for bi, b in enumerate(blks):
                po_, tpo = oset[b]
                last_i = (2 * J + b) if J < 8 else 16
                op("pe", lambda e: e.matmul(po_[0:nqb, 0:65], lhsT=pta[0:nk, bi * nqb:(bi + 1) * nqb],
                                            rhs=VP[0:nk, i, h * 65:(h + 1) * 65], start=(i == 0), stop=(i == last_i)),
                   reads=[ptt, tV[i]], writes=[tpo], mark=(i == last_i))
            if i == imax:
                nblk = 2 if J < 8 else 1
                for b in range(nblk):
                    po_, tpo = oset[b]
                    op("dve", lambda e: e.reciprocal(out=RR[0:nqb, h:h + 1], in_=po_[0:nqb, 64:65]), reads=[tpo], writes=[tRR])
                    op("dve", lambda e: e.tensor_scalar(out=AT[b][0][0:nqb, h * 64:(h + 1) * 64], in0=po_[0:nqb, 0:64], scalar1=RR[0:nqb, h:h + 1],
                                                        scalar2=None, op0=ALU.mult), reads=[tpo, tRR], writes=[AT[b][1]])
                if h == 7:
                    for b in range(nblk):
                        tq0 = 256 * J + 128 * b
                        ata_, att = AT[b]
                        ana, ant = AN[b]
                        op("act", lambda e: e.activation(out=ana[0:nqb, :], in_=ata_[0:nqb, :], func=AF.Square, accum_out=SSq[0:nqb, b:b + 1]),
                           reads=[att], writes=[ant, tSS])
                        op("act", lambda e: e.activation(out=SSq[0:nqb, b:b + 1], in_=SSq[0:nqb, b:b + 1], func=AF.Ln, bias=EPSB[0:nqb, :], scale=1.0 / 512),
                           reads=[tSS, tC], writes=[tSS])
                        op("act", lambda e: e.activation(out=SSq[0:nqb, b:b + 1], in_=SSq[0:nqb, b:b + 1], func=AF.Exp, scale=-0.5), reads=[tSS], writes=[tSS])
                        op("dve", lambda e: e.scalar_tensor_tensor(out=ana[0:nqb, :], in0=ata_[0:nqb, :], scalar=SSq[0:nqb, b:b + 1], in1=GAt[par][0:nqb, :],
                                                                   op0=ALU.mult, op1=ALU.mult), reads=[att, tSS, tGAt], writes=[ant])
                        deferred.append([4 + b, (J, b, nqb, tq0)])

        def tick():
            for d in deferred:
                d[0] -= 1
            while deferred and deferred[0][0] <= 0:
                _, a = deferred.pop(0)
                epilogue_pe(*a)

        n_items = len(items)
        for t in range(n_items + LA):
            if t < n_items:
                emit_S(t)
            if t - LA >= 0:
                emit_rest(t - LA)
            tick()
        while deferred:
            _, a = deferred.pop(0)
            epilogue_pe(*a)

    def phase_out(l, par, with_norm=True):
        j0 = U[(l, "o", 0)]
        wadvance(j0)
        Wo = [wunit(j0 + u) for u in range(4)]
        rot = Rot([0, 1, 2, 3])
        rotn = Rot([4, 5])
        def main(ti):
            t0, n = TT512[ti]
            for m in range(8):
                wa, wt = Wo[m // 2]
                mm = m % 2
                ps, tps = rot.next()
                for kc in range(8):
                    if kc < 4:
                        rhs, rt = ZT[:, kc, t0:t0 + n], tl(tZ, kc, t0, n)
                    else:
                        rhs, rt = MR[:, kc - 4, t0:t0 + n], tl(tM, kc - 4, t0, n)
                    op("pe", lambda e: e.matmul(ps[:, 0:n], lhsT=wa[:, kc, mm * 128:mm * 128 + 128], rhs=rhs, start=(kc == 0), stop=(kc == 7)),
                       reads=[wt] + rt, writes=[tps], mark=(kc == 7))
                op("dve", lambda e: e.tensor_tensor(out=H[:, m, t0:t0 + n], in0=ps[:, 0:n], in1=H[:, m, t0:t0 + n], op=ALU.add),
                   reads=[tps] + tl(tH, m, t0, n), writes=tl(tH, m, t0, n))

        ndone = [0]

        def stages(ti):
            t0, n = TT512[ti]
            return [lambda: norm_range(t0, n, par, 44, rotn, ndone)] if with_norm else []

        run_pipelined(len(TT512), main, stages)

    def phase_mlp(l, after_tile=None, setup=None):
        AR.reset(tk)
        RL = [AR.take([128, 512], F32, "RL%d" % i) for i in range(4)]
        if setup is not None:
            setup()
        rot = Rot([0, 1, 2, 3])
        rot2 = Rot([4, 5, 6, 7])
        k = 0
        for g in range(4):
            for u in range(4):
                j = U[(l, "up", g, u)]
                wadvance(j)
                wa, wt = wunit(j)
                for mm in range(2):
                    m = 2 * u + mm
                    for (t0, n) in TT512:
                        ps, tps = rot.next()
                        for kc in range(8):
                            op("pe", lambda e: e.matmul(ps[:, 0:n], lhsT=wa[:, kc, mm * 128:mm * 128 + 128], rhs=ZT[:, kc, t0:t0 + n],
                                                        start=(kc == 0), stop=(kc == 7)),
                               reads=[wt] + tl(tZ, kc, t0, n), writes=[tps], mark=(kc == 7))
                        ra, rt = RL[k % 4]
                        k += 1
                        op("act", lambda e: e.activation(out=ra[:, 0:n], in_=ps[:, 0:n], func=AF.Relu), reads=[tps], writes=[rt])
                        op("dve", lambda e: e.tensor_tensor(out=QK[:, m, t0:t0 + n], in0=ra[:, 0:n], in1=ra[:, 0:n], op=ALU.mult),
                           reads=[rt], writes=tl(tQ, m, t0, n))

            def down(m, t0, n, wa, wt, mm):
                ps, tps = rot2.next()
                for kc in range(8):
                    op("pe", lambda e: e.matmul(ps[:, 0:n], lhsT=wa[:, kc, mm * 128:mm * 128 + 128], rhs=QK[:, kc, t0:t0 + n],
                                                start=(kc == 0), stop=(kc == 7)),
                       reads=[wt] + tl(tQ, kc, t0, n), writes=[tps], mark=(kc == 7))
                op("dve", lambda e: e.tensor_tensor(out=H[:, m, t0:t0 + n], in0=ps[:, 0:n], in1=H[:, m, t0:t0 + n], op=ALU.add),
                   reads=[tps] + tl(tH, m, t0, n), writes=tl(tH, m, t0, n))

            if g < 3 or after_tile is None:
                for u in range(4):
                    j = U[(l, "dn", g, u)]
                    wadvance(j)
                    wa, wt = wunit(j)
                    for mm in range(2):
                        for (t0, n) in TT512:
                            down(2 * u + mm, t0, n, wa, wt, mm)
            else:
                jd = U[(l, "dn", g, 0)]
                wadvance(jd)
                Wd = [wunit(jd + u) for u in range(4)]
                def main(ti):
                    t0, n = TT512[ti]
                    for m in range(8):
                        wa, wt = Wd[m // 2]
                        down(m, t0, n, wa, wt, m % 2)

                run_pipelined(len(TT512), main, after_tile)

    sstate = {}
    qdone = {}

    def run_pipelined(ntiles, emit_main, stages_for):
        q = []

        def age():
            for ent in q:
                k = ent[0]
                if k < len(ent[1]):
                    ent[1][k]()
                ent[0] += 1

        for ti in range(ntiles):
            emit_main(ti)
            age()
            q.append([0, stages_for(ti)])
        while any(ent[0] < len(ent[1]) for ent in q):
            age()

    def store_setup(par, reset):
        if reset:
            AR.reset(tk)
        sstate["st"] = [AR.take([128, 1024], F32, "st%d" % i) for i in range(2)]
        sstate["rot"] = Rot([0, 1, 2])
        sstate["par"] = par
        sstate["k"] = 0
        sstate["bi"] = 0
        if last:
            sstate["sq"] = [AR.take([128, 512], BF16, "sq%d" % i) for i in range(3)]
            sstate["rs"] = [AR.take([128, 512], F32, "rs%d" % i) for i in range(2)]
            sstate["Y"] = [AR.take([128, 8, 128], F32, "Y%d" % i) for i in range(2)]
            sstate["rotn"] = Rot([3])

    def store_tile_a(ti):
        t0, n = TT512[ti]
        if last:
            ps, tps = sstate["rotn"].next()
            for kc in range(8):
                sqa, sqt = sstate["sq"][sstate["k"] % 3]
                sstate["k"] += 1
                op("act", lambda e: e.activation(out=sqa[:, 0:n], in_=H[:, kc, t0:t0 + n], func=AF.Square), reads=tl(tH, kc, t0, n), writes=[sqt])
                op("pe", lambda e: e.matmul(ps[:, 0:n], lhsT=ones_bf, rhs=sqa[:, 0:n], start=(kc == 0), stop=(kc == 7)),
                   reads=[sqt, tC], writes=[tps])
            ra, rt = sstate["rs"][ti % 2]
            op("act", lambda e: e.activation(out=ra[:, 0:n], in_=ps[:, 0:n], func=AF.Ln, bias=EPSB, scale=1.0 / D), reads=[tps, tC], writes=[rt])
            op("act", lambda e: e.activation(out=ra[:, 0:n], in_=ra[:, 0:n], func=AF.Exp, scale=-0.5), reads=[rt], writes=[rt])

    def store_tile_b(ti):
        t0, n = TT512[ti]
        par = sstate["par"]
        st = sstate["st"]
        rot = sstate["rot"]
        if last:
            ra, rt = sstate["rs"][ti % 2]
        for b0 in range(0, n, 128):
            nb = min(128, n - b0)
            tb = t0 + b0
            bi = sstate["bi"]
            sstate["bi"] += 1
            sa, stl = st[bi % 2]
            if last:
                ya, yt = sstate["Y"][bi % 2]
                for kc in range(8):
                    op("dve", lambda e: e.scalar_tensor_tensor(out=ya[:, kc, 0:nb], in0=H[:, kc, tb:tb + nb], scalar=PPt[par][:, 52 + kc:53 + kc],
                                                               in1=ra[:, b0:b0 + nb], op0=ALU.mult, op1=ALU.mult),
                       reads=tl(tH, kc, tb, nb) + [rt, tPar[par]], writes=[yt])
            for g in range(2):
                pso, tpso = rot.next()
                pso3 = pso.rearrange("p (a b) -> p a b", b=128)
                for j in range(4):
                    kc = 4 * g + j
                    if last:
                        src, srt = ya[:, kc, 0:nb], [yt]
                    else:
                        src, srt = H[:, kc, tb:tb + nb], tl(tH, kc, tb, nb)
                    op("pe", lambda e: e.transpose(pso3[0:nb, j, :], src, id_f), reads=srt + [tC], writes=[tpso], mark=(j == 3))
                dst = sa[0:nb, 512 * g:512 * (g + 1)].rearrange("p (a b) -> p a b", b=128)
                if g == 0:
                    op("act", lambda e: e.activation(out=dst, in_=pso3[0:nb, :, :], func=AF.Copy), reads=[tpso], writes=[stl])
                else:
                    op("dve", lambda e: e.tensor_copy(out=dst, in_=pso3[0:nb, :, :]), reads=[tpso], writes=[stl])
            if last:
                if tb == 0:
                    dma("sp", lambda e: e.dma_start(out=hout[0:nb - NMETA, :], in_=sa[NMETA:nb, :]), reads=[stl])
                else:
                    dma("sp", lambda e: e.dma_start(out=hout[tb - NMETA:tb - NMETA + nb, :], in_=sa[0:nb, :]), reads=[stl])
            else:
                dma("sp", lambda e: e.dma_start(out=hout[tb:tb + nb, :], in_=sa[0:nb, :]), reads=[stl])

    def phase_store(par):
        store_setup(par, True)
        for ti in range(len(TT512)):
            store_tile_a(ti)
            store_tile_b(ti)

    def dump(name, buf, tiles, nch):
        if not dbg:
            return
        tk.barrier()
        rt = []
        for c in range(nch):
            rt += tiles[c]
        dma("sp", lambda e: e.dma_start(out=dbg_out[name], in_=buf), reads=rt)

    load_params(0)
    wadvance(0)
    stored = False
    phase_load(with_norm=not dbg)
    for l in range(nl):
        par = l % 2
        if l == 0 and upto >= 1 and dbg:
            phase_norm(par, 36)
        if l == 0 and upto >= 1:
            dump("d_zt", ZT, tZ, 8)
        if upto >= 2:
            phase_rec(l, par)
        if l == 0 and upto >= 2:
            dump("d_mr", MR, tM, 4)
        if upto >= 3:
            phase_attn(l, par)
        if l == 0 and upto >= 3:
            dump("d_qk", QK, tQ, 8)
            dump("d_ma", ZT, tZ, 8)
        if upto >= 4:
            phase_out(l, par, with_norm=(upto >= 5) and not dbg)
        if l == 0:
            dump("d_h1", H, tH, 8)
        if upto >= 5 and dbg:
            phase_norm(par, 44)
        if l + 1 < nl:
            load_params(l + 1)
        if upto >= 6:
            if l + 1 < nl:
                rotn = Rot([0, 1])
                ndone1 = [0]
                phase_mlp(l, after_tile=lambda ti, p2=(l + 1) % 2, nd=ndone1: [lambda: norm_range(TT512[ti][0], TT512[ti][1], p2, 36, rotn, nd)])
            elif dbg:
                phase_mlp(l)
            else:
                phase_mlp(l, after_tile=lambda ti: [lambda: store_tile_a(ti), lambda: store_tile_b(ti)],
                          setup=lambda p2=par: store_setup(p2, False))
                stored = True
        if l == 0:
            dump("d_h2", H, tH, 8)
    if not stored:
        phase_store((nl - 1) % 2)
    tk.barrier()
    return nc


def _pp_layout(inp, l):
    pp = np.zeros((128, NPP), np.float32)
    cw = inp["conv_w"][l]
    for c in range(4):
        for k in range(4):
            pp[:, c * 4 + k] = cw[k, c * 128:(c + 1) * 128]
        pp[:, 16 + c] = inp["conv_b"][l, c * 128:(c + 1) * 128]
        pp[:, 20 + c] = inp["b_gate_a"][l, c * 128:(c + 1) * 128]
        pp[:, 24 + c] = inp["b_gate_x"][l, c * 128:(c + 1) * 128]
        pp[:, 28 + c] = inp["lru_L"][l, c * 128:(c + 1) * 128]
        pp[:, 32 + c] = inp["rec_out_g"][l, c * 128:(c + 1) * 128]
    for kc in range(8):
        pp[:, 36 + kc] = inp["attn_norm_g"][l, kc * 128:(kc + 1) * 128]
        pp[:, 44 + kc] = inp["mlp_norm_g"][l, kc * 128:(kc + 1) * 128]
        pp[:, 52 + kc] = inp["final_g"][kc * 128:(kc + 1) * 128]
    return pp


_NC_CACHE = {}


def _get_nc(nl, last, dbg=False):
    key = (nl, last, dbg)
    if key not in _NC_CACHE:
        _NC_CACHE[key] = build(nl, last, dbg)
    return _NC_CACHE[key]


def run_layers(inp, hin_list, l0, l1, last, dbg=False, ncores=8):
    f = lambda a: np.ascontiguousarray(np.asarray(a, dtype=np.float32))
    nl = l1 - l0
    nc = _get_nc(nl, last, dbg)
    pp = np.stack([_pp_layout(inp, l) for l in range(l0, l1)])
    shared = {
        "w_in": f(inp["w_in"][l0:l1]), "w_out": f(inp["w_out"][l0:l1]), "w_up": f(inp["w_up"][l0:l1]),
        "w_dn": f(inp["w_down"][l0:l1]), "w_ga": f(inp["w_gate_a"][l0:l1]), "w_gx": f(inp["w_gate_x"][l0:l1]),
        "pp": f(pp), "b_f": f(inp["b_f"][l0:l1]), "g_a": f(inp["attn_out_g"][l0:l1]),
    }
    in_maps = [dict(shared, hin=f(hin_list[b])) for b in range(ncores)]
    res = run_bass_kernel_spmd(nc, in_maps, core_ids=list(range(ncores)))
    return res


def kernel(**inputs):
    inp = {k: np.asarray(v) for k, v in inputs.items()}
    x = inp["x"].astype(np.float32)
    meta = inp["meta"].astype(np.float32)
    B = x.shape[0]
    hs = [np.concatenate([meta, x[b]], axis=0) for b in range(B)]
    l0 = 0
    while l0 < DEPTH:
        l1 = min(DEPTH, l0 + NL_PER_LAUNCH)
        last = l1 == DEPTH
        res = run_layers(inp, hs, l0, l1, last)
        hs = [res.results[b]["hout"] for b in range(B)]
        l0 = l1
    return np.stack(hs, axis=0).astype(np.float32)
```
